# Optimizing a Trainium2 kernel written in Bass

```python
import math
import jax, jax.numpy as jnp
from jax import lax
import numpy as np

D_MODEL = 1024
BATCH = 16
SEQ = 2048
DEPTH = 4
DEC_BATCH = 32
DEC_SEQ = 16
PAST_LEN = 2048

CHUNK = 64
N_PAST_CHUNKS = 8
BAND = (N_PAST_CHUNKS + 1) * CHUNK
HEAD_DIM = 64
ATT_W = D_MODEL // 2
N_HEADS = ATT_W // HEAD_DIM
CONV_CH = D_MODEL - ATT_W
N_CONV_GROUPS = CONV_CH // HEAD_DIM
MIX_W = ATT_W + CONV_CH
PROJ_W = 3 * ATT_W + 3 * CONV_CH
CONV_W = 3
REL_CLIP = 128
D_FF = ((8 * D_MODEL // 3 + 127) // 128) * 128
EPS = 1e-6
NEG_INF = -1e30
SCALE = HEAD_DIM ** -0.5

kernel_name = "hybrid_streaming_encoder_step"


def rmsnorm(x, g):
    xf = x.astype(jnp.float32)
    y = xf * lax.rsqrt(jnp.mean(xf * xf, axis=-1, keepdims=True) + EPS)
    return (y * g.astype(jnp.float32)).astype(x.dtype)


def group_rmsnorm(x, g, groups):
    shp = x.shape
    xf = x.astype(jnp.float32).reshape(shp[:-1] + (groups, shp[-1] // groups))
    y = xf * lax.rsqrt(jnp.mean(xf * xf, axis=-1, keepdims=True) + EPS)
    return (y.reshape(shp) * g.astype(jnp.float32)).astype(x.dtype)


def causal_dwconv(u_ext, w):
    T = u_ext.shape[1] - (CONV_W - 1)
    out = w[0] * u_ext[:, 0:T]
    for i in range(1, CONV_W):
        out = out + w[i] * u_ext[:, i:i + T]
    return out


def rel_bias(table, q_off, k_off):
    idx = jnp.clip(q_off[:, None] - k_off[None, :], -REL_CLIP, REL_CLIP) + REL_CLIP
    return table[:, idx]


def band_softmax(qc, kb, vb, bias, mask):
    s = jnp.einsum('bcqhd,bckhd->bchqk', qc, kb).astype(jnp.float32) + bias.astype(jnp.float32)
    s = jnp.where(mask[None, :, None, None, :], s, NEG_INF)
    p = jax.nn.softmax(s, axis=-1).astype(vb.dtype)
    return jnp.einsum('bchqk,bckhd->bcqhd', p, vb)


def chunk_band_attention(q, k, v, table):
    Bn, T, H, Dh = q.shape
    nc = T // CHUNK
    qc = (q * SCALE).reshape(Bn, nc, CHUNK, H, Dh)
    pad = jnp.zeros((Bn, N_PAST_CHUNKS, CHUNK, H, Dh), k.dtype)

    def band(t):
        tp = jnp.concatenate([pad, t.reshape(Bn, nc, CHUNK, H, Dh)], axis=1)
        return jnp.concatenate([tp[:, s:s + nc] for s in range(N_PAST_CHUNKS + 1)], axis=2)

    kb, vb = band(k), band(v)
    k_off = jnp.arange(BAND) - N_PAST_CHUNKS * CHUNK
    bias = rel_bias(table, jnp.arange(CHUNK), k_off)
    slot_ok = (jnp.arange(nc)[:, None] + jnp.arange(N_PAST_CHUNKS + 1)[None, :]) >= N_PAST_CHUNKS
    mask = jnp.repeat(slot_ok, CHUNK, axis=1)
    out = band_softmax(qc, kb, vb, bias, mask)
    return out.reshape(Bn, T, H * Dh)


def cached_band_attention(q, k, v, table, ck, cv):
    Bn, Tn, H, Dh = q.shape
    Lc = ck.shape[1]
    kb = jnp.concatenate([ck, k], axis=1)[:, None]
    vb = jnp.concatenate([cv, v], axis=1)[:, None]
    k_off = jnp.arange(Lc + Tn) - Lc
    bias = rel_bias(table, jnp.arange(Tn), k_off)
    mask = jnp.ones((1, Lc + Tn), dtype=bool)
    out = band_softmax((q * SCALE)[:, None], kb, vb, bias, mask)
    return out.reshape(Bn, Tn, H * Dh)


def trunk_layer(x, attn_fn, conv_hist, ffn_hist, ln1, w_in, rel_table, conv_w, attn_g, conv_g,
                w_out, ln2, w_up, fconv_w, fconv_b, w_down):
    Bn, T, _ = x.shape
    h = rmsnorm(x, ln1)
    proj = h @ w_in
    q = proj[..., 0:ATT_W].reshape(Bn, T, N_HEADS, HEAD_DIM)
    k = proj[..., ATT_W:2 * ATT_W].reshape(Bn, T, N_HEADS, HEAD_DIM)
    v = proj[..., 2 * ATT_W:3 * ATT_W].reshape(Bn, T, N_HEADS, HEAD_DIM)
    o = 3 * ATT_W
    bg = proj[..., o:o + CONV_CH]
    cg = proj[..., o + CONV_CH:o + 2 * CONV_CH]
    hv = proj[..., o + 2 * CONV_CH:o + 3 * CONV_CH]
    att = attn_fn(q, k, v, rel_table)
    u_ext = jnp.concatenate([conv_hist, cg * hv], axis=1)
    z = bg * causal_dwconv(u_ext, conv_w)
    mixed = jnp.concatenate([group_rmsnorm(att, attn_g, N_HEADS),
                             group_rmsnorm(z, conv_g, N_CONV_GROUPS)], axis=-1) @ w_out
    x = x + mixed
    up = rmsnorm(x, ln2) @ w_up
    up_ext = jnp.concatenate([ffn_hist, up], axis=1)
    a = causal_dwconv(up_ext, fconv_w) + fconv_b
    x = x + (jax.nn.silu(a[..., :D_FF]) * a[..., D_FF:]) @ w_down
    return x, k, v, u_ext[:, -(CONV_W - 1):], up_ext[:, -(CONV_W - 1):]


def setup_inputs(seed: int = 0) -> dict:
    key = jax.random.key(seed)
    ks = jax.random.split(key, 24)
    f32 = jnp.float32
    att_cache = min(N_PAST_CHUNKS * CHUNK, PAST_LEN)
    n = lambda i, shape, s: jax.random.normal(ks[i], shape, f32) * s
    return {
        "x_prompt": n(0, (BATCH, SEQ, D_MODEL), 1.0),
        "x_sample": n(1, (DEC_BATCH, DEC_SEQ, D_MODEL), 1.0),
        "cache_attn_k": n(2, (DEPTH, DEC_BATCH, att_cache, N_HEADS, HEAD_DIM), 1.0),
        "cache_attn_v": n(3, (DEPTH, DEC_BATCH, att_cache, N_HEADS, HEAD_DIM), 1.0),
        "state_mix_conv": n(4, (DEPTH, DEC_BATCH, CONV_W - 1, CONV_CH), 1.0),
        "state_ffn_conv": n(5, (DEPTH, DEC_BATCH, CONV_W - 1, 2 * D_FF), 1.0),
        "ln1": 1.0 + n(6, (DEPTH, D_MODEL), 0.02),
        "w_in": n(7, (DEPTH, D_MODEL, PROJ_W), D_MODEL ** -0.5),
        "rel_table": n(8, (DEPTH, N_HEADS, 2 * REL_CLIP + 1), 0.2),
        "conv_w": n(9, (DEPTH, CONV_W, CONV_CH), CONV_W ** -0.5),
        "attn_g": 1.0 + n(10, (DEPTH, ATT_W), 0.02),
        "conv_g": 1.0 + n(11, (DEPTH, CONV_CH), 0.02),
        "w_out": n(12, (DEPTH, MIX_W, D_MODEL), MIX_W ** -0.5),
        "ln2": 1.0 + n(13, (DEPTH, D_MODEL), 0.02),
        "w_up": n(14, (DEPTH, D_MODEL, 2 * D_FF), D_MODEL ** -0.5),
        "fconv_w": n(15, (DEPTH, CONV_W, 2 * D_FF), CONV_W ** -0.5),
        "fconv_b": n(16, (DEPTH, 2 * D_FF), 0.01),
        "w_down": n(17, (DEPTH, D_FF, D_MODEL), D_FF ** -0.5),
        "final_norm": 1.0 + n(18, (D_MODEL,), 0.02),
    }


def reference(x_prompt, x_sample, cache_attn_k, cache_attn_v, state_mix_conv, state_ffn_conv,
              ln1, w_in, rel_table, conv_w, attn_g, conv_g, w_out, ln2, w_up, fconv_w, fconv_b,
              w_down, final_norm):
    Bp, Tp, _ = x_prompt.shape
    keep = min(N_PAST_CHUNKS * CHUNK, Tp)
    xp, xs = x_prompt, x_sample
    kp_l, vp_l, cp_l, fp_l = [], [], [], []
    ks_l, vs_l, cs_l, fs_l = [], [], [], []
    for l in range(DEPTH):
        w = (ln1[l], w_in[l], rel_table[l], conv_w[l], attn_g[l], conv_g[l], w_out[l],
             ln2[l], w_up[l], fconv_w[l], fconv_b[l], w_down[l])
        zc = jnp.zeros((Bp, CONV_W - 1, CONV_CH), xp.dtype)
        zf = jnp.zeros((Bp, CONV_W - 1, 2 * D_FF), xp.dtype)
        xp, kp, vp, cp, fp = trunk_layer(xp, chunk_band_attention, zc, zf, *w)
        kp_l.append(kp[:, Tp - keep:])
        vp_l.append(vp[:, Tp - keep:])
        cp_l.append(cp)
        fp_l.append(fp)
        ck, cv = cache_attn_k[l], cache_attn_v[l]
        samp_attn = lambda q, k, v, t, ck=ck, cv=cv: cached_band_attention(q, k, v, t, ck, cv)
        xs, kk, vv, cs, fs = trunk_layer(xs, samp_attn, state_mix_conv[l], state_ffn_conv[l], *w)
        ks_l.append(kk)
        vs_l.append(vv)
        cs_l.append(cs)
        fs_l.append(fs)
    y_prompt = rmsnorm(xp, final_norm)
    y_sample = rmsnorm(xs, final_norm)
    return (y_prompt, y_sample,
            jnp.stack(kp_l), jnp.stack(vp_l), jnp.stack(cp_l), jnp.stack(fp_l),
            jnp.stack(ks_l), jnp.stack(vs_l), jnp.stack(cs_l), jnp.stack(fs_l))
```

```python
import bisect
import numpy as np
from contextlib import ExitStack
import concourse.bass as bass
import concourse.mybir as mybir
from concourse.bass_utils import run_bass_kernel_spmd

F32 = mybir.dt.float32
BF16 = mybir.dt.bfloat16
ALU = mybir.AluOpType
AF = mybir.ActivationFunctionType

NCORES = 8
L = 4
D = 1024
KC = 8
SEQ = 2048
NH = 8
DFF = 2816
NCH = 22
PROJ = 3072
EPS = 1e-6
NSTREAM = 4
TS = 16
NTS = NSTREAM * TS
LC = 512
NEG = -30000.0
UNITS = [("p", 0), ("ps", 1)]
NLAYERS = L


def _pmap():
    m = {}
    o = 0
    for name, n in (("ln1", L * 8), ("ln2", L * 8), ("attn_g", L * 4), ("conv_g", L * 4),
                    ("conv_w", L * 3 * 4), ("fconv_w", L * 3 * 44), ("fconv_b", L * 44),
                    ("final", 8), ("chb", L * 8)):
        m[name] = o
        o += n
    return m, o
PM, NPRM = _pmap()


class Sync:
    EPOCH = 12000

    def __init__(self, nc, es):
        self.nc = nc
        self.es = es
        self.eng = ["pe", "act", "dve", "pool", "sp"]
        self.prog = {e: [] for e in self.eng}
        self.seq = {e: 0 for e in self.eng}
        self.marks = {e: [] for e in self.eng}
        self.msems = {e: [] for e in self.eng}
        self.waited = {e: {} for e in self.eng}
        self.keys = {}
        self.dsem = {}

    def _newsem(self, name):
        return self.es.enter_context(self.nc.semaphore(name))

    def _k(self, key):
        if key not in self.keys:
            self.keys[key] = dict(w=None, r={}, dw=0, dr=0)
        return self.keys[key]

    def _dsem(self, key, kind):
        n = "d%s_%s" % (kind, key)
        if n not in self.dsem:
            self.dsem[n] = self._newsem(n)
        return n, self.dsem[n]

    def _mark_sem(self, e, idx):
        ep = idx // self.EPOCH
        while len(self.msems[e]) <= ep:
            self.msems[e].append(self._newsem("m_%s_%d" % (e, len(self.msems[e]))))
        return "m_%s_%d" % (e, ep), self.msems[e][ep], idx % self.EPOCH + 1

    def _need_compute(self, e, f, s, waits):
        if f == e and e == "pe":
            return
        i = bisect.bisect_left(self.marks[f], s)
        if i >= len(self.marks[f]):
            raise RuntimeError("no mark at/after seq %d on %s (needed by %s)" % (s, f, e))
        name, sem, val = self._mark_sem(f, i)
        for ep in range(i // self.EPOCH):
            self.waited[e]["m_%s_%d" % (f, ep)] = self.EPOCH
        if self.waited[e].get(name, 0) < val:
            self.waited[e][name] = val
            waits.append((sem, val))

    def _need_dma(self, e, key, kind, waits):
        st = self._k(key)
        cnt = st["dw"] if kind == "w" else st["dr"]
        if cnt == 0:
            return
        name, sem = self._dsem(key, kind)
        val = 16 * cnt
        if self.waited[e].get(name, 0) < val:
            self.waited[e][name] = val
            waits.append((sem, val))

    def _deps(self, e, reads, writes):
        waits = []
        for k in reads:
            st = self._k(k)
            if st["w"] is not None:
                self._need_compute(e, st["w"][0], st["w"][1], waits)
            self._need_dma(e, k, "w", waits)
        for k in writes:
            st = self._k(k)
            if st["w"] is not None:
                self._need_compute(e, st["w"][0], st["w"][1], waits)
            for f, s in st["r"].items():
                self._need_compute(e, f, s, waits)
            self._need_dma(e, k, "w", waits)
            self._need_dma(e, k, "r", waits)
        return waits

    @staticmethod
    def _isbank(k):
        return len(k) == 2 and k[0] == "b" and k[1].isdigit()

    def op(self, e, fn, reads=(), writes=(), mark=True):
        bk = [k for k in reads if self._isbank(k)]
        if bk:
            reads = [k for k in reads if not self._isbank(k)]
            writes = list(writes) + [k for k in bk if k not in writes]
        waits = self._deps(e, reads, writes)
        s = self.seq[e]
        self.seq[e] += 1
        for k in reads:
            self._k(k)["r"][e] = s
        for k in writes:
            st = self._k(k)
            st["w"] = (e, s)
            st["r"] = {}
        inc = None
        if mark:
            idx = len(self.marks[e])
            self.marks[e].append(s)
            _, sem, _ = self._mark_sem(e, idx)
            inc = sem
        self.prog[e].append((waits, fn, inc, 1))

    def dma(self, e, fn, reads=(), writes=()):
        waits = self._deps(e, reads, writes)
        assert len(reads) + len(writes) == 1
        if writes:
            k = writes[0]
            st = self._k(k)
            st["dw"] += 1
            st["w"] = None
            st["r"] = {}
            _, sem = self._dsem(k, "w")
        else:
            k = reads[0]
            st = self._k(k)
            st["dr"] += 1
            _, sem = self._dsem(k, "r")
        self.prog[e].append((waits, fn, sem, 16))

    def couple(self, keys_from, keys_to):
        for k2 in keys_to:
            st2 = self._k(k2)
            for k1 in keys_from:
                st1 = self._k(k1)
                for f, q in st1["r"].items():
                    st2["r"][f] = max(st2["r"].get(f, -1), q)
                if st1["w"] is not None:
                    f, q = st1["w"]
                    st2["r"][f] = max(st2["r"].get(f, -1), q)

    def final_waits(self, e):
        waits = []
        for k, st in self.keys.items():
            if st["dr"]:
                name, sem = self._dsem(k, "r")
                waits.append((sem, 16 * st["dr"]))
        self.prog[e].append((waits, None, None, 0))

    def emit(self, e, h):
        for waits, fn, inc, amt in self.prog[e]:
            for sem, val in waits:
                h.wait_ge(sem, val)
            if fn is None:
                continue
            ins = fn(h)
            if inc is not None:
                ins.then_inc(inc, amt)


def build_nc(units=UNITS, nlayers=NLAYERS):
    nc = bass.Bass("TRN2", target_bir_lowering=False)
    dt = nc.dram_tensor
    xp = dt("xp", [2, 128, KC, SEQ], F32, kind="ExternalInput").ap()
    xs = dt("xs", [128, KC, NTS], F32, kind="ExternalInput").ap()
    ckT = dt("ckT", [L, 4, 128, NSTREAM, LC], F32, kind="ExternalInput").ap()
    cvt = dt("cvt", [L, 4, 128, NSTREAM * 4 * 2, 64], F32, kind="ExternalInput").ap()
    smc = dt("smc", [L, 128, 4, NSTREAM, 2], F32, kind="ExternalInput").ap()
    sfc = dt("sfc", [L, 128, 44, NSTREAM, 2], F32, kind="ExternalInput").ap()
    w_in = dt("w_in", [L, D, PROJ], F32, kind="ExternalInput").ap()
    w_out = dt("w_out", [L, D, D], F32, kind="ExternalInput").ap()
    w_up = dt("w_up", [L, D, 2 * DFF], F32, kind="ExternalInput").ap()
    w_down = dt("w_down", [L, DFF, D], F32, kind="ExternalInput").ap()
    prm_d = dt("prm", [128, NPRM], F32, kind="ExternalInput").ap()
    bias_d = dt("biasT", [L, 128, NH, 2, 128], F32, kind="ExternalInput").ap()
    eye_d = dt("eye", [128, 128], F32, kind="ExternalInput").ap()

    yp = dt("yp", [2, 128, KC, SEQ], F32, kind="ExternalOutput").ap()
    ys = dt("ys", [128, KC, NTS], F32, kind="ExternalOutput").ap()
    kp = dt("kp", [L, 2, 128, 4, 512], F32, kind="ExternalOutput").ap()
    vp = dt("vp", [L, 2, 512, 512], F32, kind="ExternalOutput").ap()
    cpo = dt("cpo", [L, 2, 128, 4, 2], F32, kind="ExternalOutput").ap()
    fpo = dt("fpo", [L, 2, 128, 44, 2], F32, kind="ExternalOutput").ap()
    kso = dt("kso", [L, 128, 4, NTS], F32, kind="ExternalOutput").ap()
    vso = dt("vso", [L, NSTREAM, TS, 512], F32, kind="ExternalOutput").ap()
    cso = dt("cso", [L, 128, 4, NSTREAM, 2], F32, kind="ExternalOutput").ap()
    fso = dt("fso", [L, 128, 44, NSTREAM, 2], F32, kind="ExternalOutput").ap()

    es = ExitStack()
    with es:
        def sb(name, shape, dtype):
            return es.enter_context(nc.sbuf_tensor(name, shape, dtype))

        W = SEQ + NTS
        SOFF = SEQ
        xT = sb("xT", [128, KC, W], F32)
        hb = sb("hb", [128, KC, W], BF16)
        mix = sb("mix", [128, 4, W], BF16)
        qs = sb("qs", [128, 2, NTS], BF16)
        cst_p = sb("cst_p", [128, 4, 2], F32)
        fst_p = sb("fst_p", [128, 44, 2], F32)
        NSLOT = 6
        ring = sb("ring", [128, NSLOT, 4096], BF16)
        scr = sb("scr", [128, 6272], BF16)
        qTz = scr[:, 0:4096].rearrange("p (h t) -> p h t", h=2)
        kT = scr[:, 4096:6272]
        vtok = sb("vtok", [128, 17 * 2 * 65], BF16)
        vnew = sb("vnew", [16, NSTREAM, 2, 65], BF16)
        prm = sb("prm_sb", [128, NPRM], F32)
        biasb = sb("biasb", [128, 2, 2, 128], BF16)
        mask4 = sb("mask4", [128, 128], BF16)
        ident = sb("ident", [128, 128], BF16)
        ones = sb("ones", [128, 128], BF16)
        blk = sb("blk", [128, 128], BF16)
        sclv = sb("sclv", [128, 1], F32)
        stg = sb("stg", [128, 512], F32)
        sq = sb("sq", [128, 2, 512], BF16)
        osb = stg[:, 0:128].bitcast(BF16).rearrange("p (h n) -> p h n", h=2)
        scrB = sb("scrB", [128, 2064], F32)
        cu = scrB[:, 0:1040].rearrange("p (a n) -> p a n", a=2)
        cs_ = scrB[:, 1040:1552]
        zt = scrB[:, 1552:2064]
        T2OFF = [0, 520, 1040]
        T2KEY = ["cu0", "cu1", "cs"]
        cst = sb("cst", [128, 4, NSTREAM, 2], F32)
        fst = sb("fst", [128, 44, NSTREAM, 2], F32)
        t1 = sb("t1", [128, 3, 512], F32)
        t1b = t1[:, :, :].rearrange("p a n -> p (a n)").bitcast(BF16)
        pT = t1b[:, 0:1280].rearrange("p (h n) -> p h n", h=2)
        Rb = t1b[:, 1280:1536].rearrange("p (h n) -> p h n", h=2)
        rstd = t1[:, :, :].rearrange("p a n -> p (a n)")[:, 768:1024].rearrange("p (h n) -> p h n", h=2)
        sel = sb("sel", [128, 2, 128], BF16)
        sclv2 = sb("sclv2", [128, 1], F32)
        AKEYS = ["pT0", "pT1", "Rb0", "Rb1", "rstd0", "rstd1"]
        BKEYS = ["t1_0", "t1_1", "t1_2"]
        epsc = sb("epsc", [128, 1], F32)
        bank = [es.enter_context(nc.psum_tensor("bank%d" % i, [128, 512], F32)) for i in range(8)]

        vtp = vtok[:, :].rearrange("p (b h e) -> p b h e", b=17, h=2)
        vcs = vtok[:, 0:NSTREAM * 4 * 2 * 65].rearrange("p (s b h e) -> p s b h e", s=NSTREAM, b=4, h=2)
        actb = scr[:, 0:4096].rearrange("p (a c t) -> p a c t", a=2, c=4)

        S = Sync(nc, es)
        block = es.enter_context(nc.Block())

        def pcol(name, idx, p0=0, p1=128):
            c = PM[name] + idx
            return prm[p0:p1, c:c + 1]

        def mm(out, lhsT, rhs, start, stop, reads, writes, mark=True):
            S.op("pe", lambda e: e.matmul(out, lhsT=lhsT, rhs=rhs, start=start, stop=stop),
                 reads=reads, writes=writes, mark=mark)

        def act(out, in_, func, reads, writes, bias=None, scale=None):
            kw = {}
            if bias is not None:
                kw["bias"] = bias
            if scale is not None:
                kw["scale"] = scale
            S.op("act", lambda e: e.activation(out=out, in_=in_, func=func, **kw), reads=reads, writes=writes)

        def dve(fn, reads, writes):
            S.op("dve", fn, reads=reads, writes=writes)

        def stt(out, in0, scalar, in1, op0, op1, reads, writes):
            dve(lambda e: e.scalar_tensor_tensor(out=out, in0=in0, scalar=scalar, in1=in1, op0=op0, op1=op1),
                reads, writes)

        def tt(out, in0, in1, op, reads, writes):
            dve(lambda e: e.tensor_tensor(out=out, in0=in0, in1=in1, op=op), reads, writes)

        def cp(out, in_, reads, writes):
            dve(lambda e: e.tensor_copy(out=out, in_=in_), reads, writes)

        def recip(out, in_, reads, writes):
            dve(lambda e: e.reciprocal(out=out, in_=in_), reads, writes)

        def mset(ap, val, key):
            dve(lambda e: e.memset(ap, val), [], [key])

        def ld(out, in_, key):
            S.dma("sp", lambda e: e.dma_start(out=out, in_=in_), writes=[key])

        def stq(out, in_, key):
            S.dma("sp", lambda e: e.dma_start(out=out, in_=in_), reads=[key])

        def wld(out, in_, key):
            S.dma("pool", lambda e: e.dma_start(out=out, in_=in_), writes=[key])

        ring_next = [0]

        def ring_alloc():
            k = ring_next[0] % NSLOT
            ring_next[0] += 1
            return k

        def w_cols(wl, c0, ncols, r0=0, nkc=KC):
            return wl[r0:r0 + nkc * 128, c0:c0 + ncols].rearrange("(kc p) n -> p kc n", p=128)

        pscnt = [0]

        def psbank():
            b = pscnt[0] % 2
            pscnt[0] += 1
            return b

        ld(prm[:], prm_d[:], "prm")
        wld(ident[:], eye_d[:], "ident")
        mset(ones[:], 1.0, "ones")
        mset(blk[:], 0.0, "blk")
        mset(blk[0:64, 0:64], 1.0, "blk")
        mset(blk[64:128, 64:128], 1.0, "blk")
        mset(mask4[:], 0.0, "mask4")
        mset(mask4[0:64, 64:128], NEG, "mask4")
        mset(sel[:], 0.0, "sel")
        mset(sel[0:65, 0, 0:64], 1.0, "sel")
        mset(sel[0:65, 1, 64:128], 1.0, "sel")
        mset(epsc[:], EPS, "epsc")
        mset(sclv2[:], 1.0, "sclv2")
        mset(sclv2[64:128, :], float(64.0 * EPS), "sclv2")
        mset(sclv[:], 1.0, "sclv")
        mset(sclv[64:128, :], float(np.sqrt(64.0 * EPS)), "sclv")
        mset(vtok[:], 1.0, "vtok")
        mset(vnew[:], 1.0, "vnew")
        mset(qs[:], 0.0, "qs")

        def norm_tiles(tiles, gname, gidx0, out_fn, outkey_fn):
            for (ti, t0, T, samp) in tiles:
                for c in range(KC):
                    par = c % 2
                    act(sq[:, par, 0:T], xT[:, c, t0:t0 + T], AF.Square, ["xT%d" % ti], ["sq%d" % par])
                    mm(bank[7][:, 0:T], ones[:], sq[:, par, 0:T], c == 0, c == KC - 1,
                       ["sq%d" % par, "ones"], ["b7"])
                act(bank[7][:, 0:T], bank[7][:, 0:T], AF.Ln, ["b7", "epsc"], ["b7"], bias=epsc[:, :], scale=1.0 / D)
                act(bank[7][:, 0:T], bank[7][:, 0:T], AF.Exp, ["b7"], ["b7"], scale=-0.5)
                for c in range(KC):
                    stt(out_fn(c, t0, T), xT[:, c, t0:t0 + T], pcol(gname, gidx0 + c), bank[7][:, 0:T],
                        ALU.mult, ALU.mult, ["xT%d" % ti, "b7", "prm"], [outkey_fn(ti)])

        def head_norm(l, j, hh, N, dst):
            pb = 64 * hh
            bo = 6 + hh
            bkey = "b%d" % bo
            o_ps = bank[bo][0:65, 0:N]
            act(Rb[0:65, hh, 0:N], o_ps, AF.Square, [bkey, "sclv"], ["Rb%d" % hh], scale=sclv[0:65, :])
            st_ps = bank[bo][pb:pb + 64, 128:128 + N]
            mm(st_ps, ones[0:65, 0:64], Rb[0:65, hh, 0:N], True, True, ["Rb%d" % hh, "ones"], [bkey])
            act(st_ps, st_ps, AF.Ln, [bkey], [bkey], scale=1.0 / 64)
            act(rstd[pb:pb + 64, hh, 0:N], st_ps, AF.Exp, [bkey], ["rstd%d" % hh], scale=-0.5)
            stt(dst, bank[bo][0:64, 0:N], pcol("attn_g", l * 4 + j, pb, pb + 64),
                rstd[pb:pb + 64, hh, 0:N], ALU.mult, ALU.mult, [bkey, "rstd%d" % hh, "prm"], ["mix"])

        def conv_taps(P, Tv, Hh, wname, widx, nseg, sl, rk_p, k_t, bcol=None, first_on_act=True):
            w0 = prm[:, PM[wname] + widx[0]:PM[wname] + widx[0] + 1]
            w1 = prm[:, PM[wname] + widx[1]:PM[wname] + widx[1] + 1]
            w2 = prm[:, PM[wname] + widx[2]:PM[wname] + widx[2] + 1]
            if bcol is not None:
                act(Tv, P, AF.Identity, rk_p + ["prm"], [k_t], bias=bcol, scale=w2)
            else:
                act(Tv, P, AF.Copy, rk_p + ["prm"], [k_t], scale=w2)
            stt(Tv[:, :, 1:sl], P[:, :, 0:sl - 1], w1, Tv[:, :, 1:sl], ALU.mult, ALU.add, rk_p + ["prm", k_t], [k_t])
            stt(Tv[:, :, 2:sl], P[:, :, 0:sl - 2], w0, Tv[:, :, 2:sl], ALU.mult, ALU.add, rk_p + ["prm", k_t], [k_t])

        def hist_taps(Tv, Hh, wname, widx, hkey, k_t):
            w0 = prm[:, PM[wname] + widx[0]:PM[wname] + widx[0] + 1]
            w1 = prm[:, PM[wname] + widx[1]:PM[wname] + widx[1] + 1]
            stt(Tv[:, :, 0:1], Hh[:, :, 1:2], w1, Tv[:, :, 0:1], ALU.mult, ALU.add, [hkey, "prm", k_t], [k_t])
            stt(Tv[:, :, 0:2], Hh[:, :, 0:2], w0, Tv[:, :, 0:2], ALU.mult, ALU.add, [hkey, "prm", k_t], [k_t])

        for (kind, sidx) in units:
            has_p = "p" in kind
            has_s = "s" in kind
            tiles = []
            if has_p:
                tiles += [(i, i * 512, 512, False) for i in range(4)]
            if has_s:
                tiles += [(4, SOFF, NTS, True)]
            parts = (["p"] if has_p else []) + (["s"] if has_s else [])
            for (ti, t0, T, samp) in tiles:
                src = xs[:, :, :] if samp else xp[sidx, :, :, t0:t0 + T]
                ld(xT[:, :, t0:t0 + T], src, "xT%d" % ti)

            for l in range(nlayers):
                import os as _os
                _sk = _os.environ.get("DBG_SKIP", "") if l >= 1 else ""
                if has_s:
                    ld(cst[:], smc[l], "cst")
                    ld(fst[:], sfc[l], "fst")

                import os as _os
                _dbg = _os.environ.get("DBG_STOP", "") if (l >= 1 or _os.environ.get("DBG_L0")) else ""
                if _dbg == "loads":
                    continue
                norm_tiles(tiles, "ln1", l * 8, lambda c, t0, T: hb[:, c, t0:t0 + T], lambda ti: "hb%d" % ti)

                if _dbg == "norm1":
                    continue
                pend_norm = []
                cvn = [0]
                cbn = [0]

                def cbank():
                    b_ = cbn[0] % 6
                    cbn[0] += 1
                    return b_
                for jc in range(4):
                    k = ring_alloc()
                    rk = "ring%d" % k
                    wc = ring[:, k, 0:1024].rearrange("p (kc n) -> p kc n", kc=KC)
                    wu = ring[:, k, 1024:2048].rearrange("p (kc n) -> p kc n", kc=KC)
                    wb = ring[:, k, 2048:3072].rearrange("p (kc n) -> p kc n", kc=KC)
                    wld(wc, w_cols(w_in[l], 2048 + jc * 128, 128), rk)
                    wld(wu, w_cols(w_in[l], 2560 + jc * 128, 128), rk)
                    wld(wb, w_cols(w_in[l], 1536 + jc * 128, 128), rk)
                    for (ti, t0, T, samp) in tiles:
                        prompt = not samp
                        nseg, sl = (NSTREAM, TS) if samp else (1, 512)
                        par = ti % 2
                        cuv = cu[:, par, 0:nseg * (sl + 2)].rearrange("p (s n) -> p s n", s=nseg)
                        b = cbank()
                        for c in range(KC):
                            mm(bank[b][:, 0:T], wc[:, c, :], hb[:, c, t0:t0 + T], c == 0, c == KC - 1,
                               [rk, "hb%d" % ti], ["b%d" % b], mark=(c == KC - 1))
                        act(cs_[:, 0:T], bank[b][:, 0:T], AF.Copy, ["b%d" % b], ["cs"])
                        b = cbank()
                        for c in range(KC):
                            mm(bank[b][:, 0:T], wu[:, c, :], hb[:, c, t0:t0 + T], c == 0, c == KC - 1,
                               [rk, "hb%d" % ti], ["b%d" % b], mark=(c == KC - 1))
                        if prompt:
                            if ti == 0:
                                mset(cuv[:, :, 0:2], 0.0, "cu%d" % par)
                            else:
                                cp(cuv[:, :, 0:2], cu[:, 1 - par, 512:514].rearrange("p (s n) -> p s n", s=1),
                                   ["cu%d" % (1 - par)], ["cu%d" % par])
                        else:
                            cp(cuv[:, :, 0:2], cst[:, jc, :, :], ["cst"], ["cu%d" % par])
                        tt(cuv[:, :, 2:sl + 2], bank[b][:, 0:T].rearrange("p (s n) -> p s n", s=nseg),
                           cs_[:, 0:T].rearrange("p (s n) -> p s n", s=nseg), ALU.mult,
                           ["b%d" % b, "cs"], ["cu%d" % par])
                        if prompt and ti == 3:
                            cp(cst_p[:, jc:jc + 1, :], cuv[:, :, sl:sl + 2], ["cu%d" % par], ["cst_p"])
                        if samp:
                            cp(cst[:, jc, :, :], cuv[:, :, sl:sl + 2], ["cu%d" % par], ["cst"])
                        zpar = cvn[0] % 2
                        cvn[0] += 1
                        zbuf = zt if zpar == 0 else stg
                        zkey = "zt" if zpar == 0 else "stg"
                        ztv = zbuf[:, 0:T].rearrange("p (s n) -> p s n", s=nseg)
                        wi = [(l * 3 + i) * 4 + jc for i in range(3)]
                        w0 = pcol("conv_w", wi[0]); w1 = pcol("conv_w", wi[1]); w2 = pcol("conv_w", wi[2])
                        act(ztv, cuv[:, :, 2:sl + 2], AF.Copy, ["cu%d" % par, "prm"], [zkey], scale=w2)
                        stt(ztv, cuv[:, :, 1:sl + 1], w1, ztv, ALU.mult, ALU.add, ["cu%d" % par, "prm", zkey], [zkey])
                        stt(ztv, cuv[:, :, 0:sl], w0, ztv, ALU.mult, ALU.add, ["cu%d" % par, "prm", zkey], [zkey])
                        b = cbank()
                        for c in range(KC):
                            mm(bank[b][:, 0:T], wb[:, c, :], hb[:, c, t0:t0 + T], c == 0, c == KC - 1,
                               [rk, "hb%d" % ti], ["b%d" % b], mark=(c == KC - 1))
                        tt(zbuf[:, 0:T], bank[b][:, 0:T], zbuf[:, 0:T], ALU.mult, ["b%d" % b, zkey], [zkey])

                        def norm_stage(zbuf=zbuf, zkey=zkey, zpar=zpar, T=T, t0=t0, jc=jc, l=l):
                            act(sq[:, zpar, 0:T], zbuf[:, 0:T], AF.Square, [zkey], ["sq%d" % zpar])
                            mm(bank[7][:, 0:T], blk[:], sq[:, zpar, 0:T], True, True, ["sq%d" % zpar, "blk"], ["b7"])
                            act(bank[7][:, 0:T], bank[7][:, 0:T], AF.Ln, ["b7", "epsc"], ["b7"], bias=epsc[:, :], scale=1.0 / 64)
                            act(bank[7][:, 0:T], bank[7][:, 0:T], AF.Exp, ["b7"], ["b7"], scale=-0.5)
                            stt(mix[:, jc, t0:t0 + T], zbuf[:, 0:T], pcol("conv_g", l * 4 + jc), bank[7][:, 0:T],
                                ALU.mult, ALU.mult, [zkey, "b7", "prm"], ["mix"])
                        if pend_norm:
                            pend_norm.pop(0)()
                        pend_norm.append(norm_stage)
                while pend_norm:
                    pend_norm.pop(0)()
                if has_p:
                    stq(cpo[l, sidx], cst_p[:], "cst_p")
                if has_s:
                    stq(cso[l], cst[:], "cst")

                def wout_part(r0):
                    k = ring_alloc()
                    rk = "ring%d" % k
                    wv_ = ring[:, k, 0:4096].rearrange("p (kc n) -> p kc n", kc=4)
                    wld(wv_, w_cols(w_out[l], 0, D, r0=r0, nkc=4), rk)
                    for (ti, t0, T, samp) in tiles:
                        for ft in range(8):
                            b = psbank()
                            for c in range(4):
                                mm(bank[b][:, 0:T], wv_[:, c, ft * 128:(ft + 1) * 128], mix[:, c, t0:t0 + T],
                                   c == 0, c == 3, [rk, "mix"], ["b%d" % b], mark=(c == 3))
                            tt(xT[:, ft, t0:t0 + T], bank[b][:, 0:T], xT[:, ft, t0:t0 + T], ALU.add,
                               ["b%d" % b, "xT%d" % ti], ["xT%d" % ti])

                if _dbg == "conv":
                    continue
                wout_part(512)
                if _dbg == "wout1":
                    continue

                S.couple(BKEYS, AKEYS)
                if has_p:
                    S.op("pool", lambda e: e.memset(qTz[64:128, 0, 0:SEQ], 0.0), writes=["qT0"])
                    S.op("pool", lambda e: e.memset(qTz[0:64, 1, 0:SEQ], 0.0), writes=["qT1"])
                for j in range(4):
                    k = ring_alloc()
                    rk = "ring%d" % k
                    wq = ring[:, k, 0:1024].rearrange("p (kc n) -> p kc n", kc=KC)
                    wk = ring[:, k, 1024:2048].rearrange("p (kc n) -> p kc n", kc=KC)
                    wv = ring[:, k, 2048:3072].rearrange("p (kc n) -> p kc n", kc=KC)
                    wld(wq, w_cols(w_in[l], j * 128, 128), rk)
                    wld(wk, w_cols(w_in[l], 512 + j * 128, 128), rk)
                    wld(wv, w_cols(w_in[l], 1024 + j * 128, 128), rk)
                    sv = stg[:, :].rearrange("p (h n) -> p h n", h=2)
                    ld(sv, bias_d[l, :, 2 * j:2 * j + 2, :, :].rearrange("p h d n -> p h (d n)"), "stg")
                    for h2 in range(2):
                        dve(lambda e, h2=h2, sv=sv, l=l, j=j: e.tensor_scalar(
                            out=biasb[:, h2, :, :].rearrange("p d n -> p (d n)"), in0=sv[:, h2, :],
                            scalar1=pcol("chb", l * 8 + 2 * j + h2), scalar2=None, op0=ALU.subtract),
                            ["stg", "prm"], ["biasb"])
                        mset(biasb[64:128, h2, 0, 0:64], NEG, "biasb")
                    for part in parts:
                        prompt = part == "p"
                        ptiles = [t_ for t_ in tiles if t_[3] != prompt]
                        koff = 0 if prompt else NSTREAM * LC
                        _dl = _os.environ.get("DBG_LD", "")
                        if not prompt and "nokt" not in _dl:
                            wld(kT[:, 0:NSTREAM * LC].rearrange("p (s n) -> p s n", s=NSTREAM), ckT[l, j], "kT")
                        if not prompt and "novcs" not in _dl:
                            wld(vtok[:, 0:NSTREAM * 4 * 2 * 65].rearrange("p (g e) -> p g e", e=65)[:, :, 0:64], cvt[l, j], "vtok")
                        for (ti, t0, T, samp) in ptiles:
                            lo = t0 - SOFF if samp else t0
                            b = psbank()
                            for c in range(KC):
                                mm(bank[b][:, 0:T], wq[:, c, :], hb[:, c, t0:t0 + T], c == 0, c == KC - 1,
                                   [rk, "hb%d" % ti], ["b%d" % b], mark=(c == KC - 1))
                            if prompt:
                                act(qTz[0:64, 0, t0:t0 + T], bank[b][0:64, 0:T], AF.Copy, ["b%d" % b], ["qT0"], scale=0.125)
                                act(qTz[64:128, 1, t0:t0 + T], bank[b][64:128, 0:T], AF.Copy, ["b%d" % b], ["qT1"], scale=0.125)
                            else:
                                act(qs[0:64, 0, 0:T], bank[b][0:64, 0:T], AF.Copy, ["b%d" % b], ["qs"], scale=0.125)
                                act(qs[64:128, 1, 0:T], bank[b][64:128, 0:T], AF.Copy, ["b%d" % b], ["qs"], scale=0.125)
                            b = psbank()
                            for c in range(KC):
                                mm(bank[b][:, 0:T], wk[:, c, :], hb[:, c, t0:t0 + T], c == 0, c == KC - 1,
                                   [rk, "hb%d" % ti], ["b%d" % b], mark=(c == KC - 1))
                            act(kT[:, koff + lo:koff + lo + T], bank[b][:, 0:T], AF.Copy, ["b%d" % b], ["kT"])
                            if (not prompt) or ti == 3:
                                cp(stg[:, 0:T], bank[b][:, 0:T], ["b%d" % b], ["stg"])
                                if prompt:
                                    stq(kp[l, sidx, :, j, :], stg[:, 0:T], "stg")
                                else:
                                    stq(kso[l, :, j, :], stg[:, 0:T], "stg")
                            if "nov" in _dl:
                                continue
                            if prompt:
                                for tb in range(4):
                                    g = ti * 4 + tb
                                    b = psbank()
                                    for c in range(KC):
                                        mm(bank[b][:, 0:128], hb[:, c, t0 + tb * 128:t0 + (tb + 1) * 128], wv[:, c, :],
                                           c == 0, c == KC - 1, [rk, "hb%d" % ti], ["b%d" % b], mark=(c == KC - 1))
                                    act(vtp[:, g, :, 0:64], bank[b][:, 0:128].rearrange("p (h e) -> p h e", h=2),
                                        AF.Copy, ["b%d" % b], ["vtok"])
                                    if ti == 3:
                                        cp(stg[:, 0:128], bank[b][:, 0:128], ["b%d" % b], ["stg"])
                                        stq(vp[l, sidx, tb * 128:(tb + 1) * 128, j * 128:(j + 1) * 128], stg[:, 0:128], "stg")
                            else:
                                for s_ in range(NSTREAM):
                                    b = psbank()
                                    for c in range(KC):
                                        mm(bank[b][0:TS, 0:128], hb[:, c, t0 + s_ * TS:t0 + (s_ + 1) * TS], wv[:, c, :],
                                           c == 0, c == KC - 1, [rk, "hb%d" % ti], ["b%d" % b], mark=(c == KC - 1))
                                    act(vnew[0:TS, s_, :, 0:64], bank[b][0:TS, 0:128].rearrange("p (h e) -> p h e", h=2),
                                        AF.Copy, ["b%d" % b], ["vnew"])
                                    cp(stg[0:TS, 0:128], bank[b][0:TS, 0:128], ["b%d" % b], ["stg"])
                                    stq(vso[l, s_, :, j * 128:(j + 1) * 128], stg[0:TS, 0:128], "stg")
                        _da = _os.environ.get("DBG_ATT", "")
                        if _da == "noattn":
                            continue
                        if prompt:
                            def stA(i, hh):
                                nkb = min(i, 4) + 1
                                h = 2 * j + hh
                                pb = 64 * hh
                                for d in range(nkb):
                                    jj = i - d
                                    bk = 2 + 2 * hh + d // 4
                                    o_ = bank[bk][:, (d % 4) * 128:(d % 4) * 128 + 128]
                                    hasb = d in (0, 1, 4)
                                    mm(o_, kT[:, jj * 128:(jj + 1) * 128], qTz[:, hh, i * 128:(i + 1) * 128],
                                       True, not hasb, ["kT", "qT%d" % hh], ["b%d" % bk], mark=not hasb)
                                    if hasb:
                                        rb_ = mask4[:, :] if d == 4 else biasb[:, hh, d, :]
                                        mm(o_, ident[:], rb_, False, True, ["ident", "biasb", "mask4"], ["b%d" % bk])
                                n0 = min(nkb, 4) * 128
                                bk = 2 + 2 * hh
                                act(pT[:, hh, 0:n0], bank[bk][:, 0:n0], AF.Exp, ["b%d" % bk], ["pT%d" % hh])
                                if nkb == 5:
                                    act(pT[:, hh, 512:640], bank[bk + 1][:, 0:128], AF.Exp, ["b%d" % (bk + 1)], ["pT%d" % hh])

                            PVB = [0, 1, 6, 7]

                            def stB(i, hh):
                                nkb = min(i, 4) + 1
                                bo = PVB[(2 * i + hh) % 4]
                                for d in range(nkb):
                                    jj = i - d
                                    mm(bank[bo][0:65, 0:128], vtp[:, jj, hh, :],
                                       pT[:, hh, d * 128:(d + 1) * 128], d == 0, d == nkb - 1,
                                       ["vtok", "pT%d" % hh], ["b%d" % bo], mark=(d == nkb - 1))
                                cp(osb[0:65, hh, :], bank[bo][0:65, 0:128], ["b%d" % bo], ["stg"])
                                stt(Rb[0:65, hh, 0:128], bank[bo][0:65, 0:128], sclv2[0:65, :], osb[0:65, hh, :],
                                    ALU.mult, ALU.mult, ["b%d" % bo, "stg", "sclv2"], ["Rb%d" % hh])

                            def stC2(i):
                                bs = PVB[(2 * i + 1) % 4]
                                skey = "b%d" % bs
                                rp = i % 2
                                for hh in range(2):
                                    mm(bank[bs][:, 128:256], sel[0:65, hh, :], Rb[0:65, hh, 0:128], hh == 0, hh == 1,
                                       ["Rb%d" % hh, "sel"], [skey], mark=(hh == 1))
                                act(bank[bs][:, 128:256], bank[bs][:, 128:256], AF.Ln, [skey], [skey], scale=1.0 / 64)
                                act(rstd[:, rp, 0:128], bank[bs][:, 128:256], AF.Exp, [skey], ["rstd%d" % rp], scale=-0.5)
                                for hh in range(2):
                                    pb = 64 * hh
                                    bo = PVB[(2 * i + hh) % 4]
                                    stt(mix[pb:pb + 64, j, i * 128:(i + 1) * 128], bank[bo][0:64, 0:128],
                                        pcol("attn_g", l * 4 + j, pb, pb + 64), rstd[pb:pb + 64, rp, 0:128],
                                        ALU.mult, ALU.mult, ["b%d" % bo, "rstd%d" % rp, "prm"], ["mix"])

                            seq_ = [(i, hh) for i in range(16) for hh in range(2)]
                            for n in range(len(seq_) + 2):
                                if n < len(seq_):
                                    stA(*seq_[n])
                                if 0 <= n - 2 < len(seq_) and (n - 2) % 2 == 1:
                                    stC2((n - 2) // 2)
                                if 0 <= n - 1 < len(seq_):
                                    stB(*seq_[n - 1])
                        else:
                            for hh in range(2):
                                h = 2 * j + hh
                                pb = 64 * hh
                                bk = 2 + 2 * hh
                                Sv = bank[bk][:, 0:NSTREAM * 5 * TS].rearrange("p (s b q) -> p s b q", s=NSTREAM, b=5)
                                pv = pT[:, hh, 0:NSTREAM * 5 * TS].rearrange("p (s b q) -> p s b q", s=NSTREAM, b=5)
                                for s_ in range(NSTREAM):
                                    qs_ = qs[pb:pb + 64, hh, s_ * TS:(s_ + 1) * TS]
                                    for kb in range(4):
                                        mm(Sv[:, s_, kb, :], kT[pb:pb + 64, s_ * LC + kb * 128:s_ * LC + (kb + 1) * 128], qs_,
                                           True, kb != 3, ["kT", "qs"], ["b%d" % bk], mark=(kb != 3))
                                        if kb == 3:
                                            mm(Sv[:, s_, kb, :], ident[:], biasb[:, hh, 1, 0:TS], False, True,
                                               ["ident", "biasb"], ["b%d" % bk])
                                    mm(Sv[0:TS, s_, 4, :], kT[pb:pb + 64, NSTREAM * LC + s_ * TS:NSTREAM * LC + (s_ + 1) * TS], qs_,
                                       True, False, ["kT", "qs"], ["b%d" % bk], mark=False)
                                    mm(Sv[0:TS, s_, 4, :], ident[0:TS, 0:TS], biasb[0:TS, hh, 0, 0:TS], False, True,
                                       ["ident", "biasb"], ["b%d" % bk])
                                if _da == "S":
                                    continue
                                act(pv[:, :, 0:4, :], Sv[:, :, 0:4, :], AF.Exp, ["b%d" % bk], ["pT%d" % hh])
                                act(pv[0:TS, :, 4:5, :], Sv[0:TS, :, 4:5, :], AF.Exp, ["b%d" % bk], ["pT%d" % hh])
                                if _da == "exp":
                                    continue
                                for s_ in range(NSTREAM):
                                    o_ = bank[6 + hh][0:65, s_ * TS:(s_ + 1) * TS]
                                    for kb in range(4):
                                        mm(o_, vcs[:, s_, kb, hh, :], pv[:, s_, kb, :], kb == 0, False,
                                           ["vtok", "pT%d" % hh], ["b%d" % (6 + hh)], mark=False)
                                    mm(o_, vnew[0:TS, s_, hh, :], pv[0:TS, s_, 4, :], False, True,
                                       ["vnew", "pT%d" % hh], ["b%d" % (6 + hh)])
                                if _da == "pv":
                                    continue
                                head_norm(l, j, hh, NTS, mix[pb:pb + 64, j, SOFF:SOFF + NTS])

                if _dbg == "attn":
                    continue
                wout_part(0)
                if _dbg == "wout0":
                    continue

                S.couple(AKEYS, BKEYS)
                norm_tiles(tiles, "ln2", l * 8, lambda c, t0, T: hb[:, c, t0:t0 + T], lambda ti: "hb%d" % ti)
                slices = [(0, 4), (4, 4), (8, 4), (12, 4), (16, 4), (20, 2)]
                if _dbg == "norm2":
                    continue
                if _dbg == "ffn1":
                    slices = slices[:1]
                acnt = 0
                ecnt = 0
                pending_down = []
                ftiles = []
                if has_p:
                    ftiles += [(0, 512, 0, False), (510, 512, 2, False), (1020, 512, 2, False), (1530, 456, 2, False),
                               (1984, 64, 2, False)]
                if has_s:
                    ftiles += [(SOFF, NTS, 0, True)]
                sl_w = {}

                def load_slice(si):
                    c0_, nch_ = slices[si]
                    kg_ = ring_alloc(); kv_ = ring_alloc(); kd_ = ring_alloc()
                    Wg_ = ring[:, kg_, 0:KC * nch_ * 128].rearrange("p (kc n) -> p kc n", kc=KC)
                    Wv_ = ring[:, kv_, 0:KC * nch_ * 128].rearrange("p (kc n) -> p kc n", kc=KC)
                    Wd_ = ring[:, kd_, 0:nch_ * D].rearrange("p (c n) -> p c n", c=nch_)
                    wld(Wg_, w_cols(w_up[l], c0_ * 128, nch_ * 128), "ring%d" % kg_)
                    wld(Wv_, w_cols(w_up[l], DFF + c0_ * 128, nch_ * 128), "ring%d" % kv_)
                    wld(Wd_, w_down[l, c0_ * 128:(c0_ + nch_) * 128, :].rearrange("(c p) n -> p c n", p=128), "ring%d" % kd_)
                    sl_w[si] = (kg_, kv_, kd_, Wg_, Wv_, Wd_)

                load_slice(0)
                for si, (c0, nch) in enumerate(slices):
                    kg, kv, kd, Wg, Wv, Wd = sl_w[si]
                    for fi, (f0, T, skip, samp) in enumerate(ftiles):
                        prompt = not samp
                        nseg, sl = (NSTREAM, TS) if samp else (1, 512)
                        apar = acnt % 2
                        acnt += 1
                        akey = "qT%d" % apar
                        To = T - skip
                        o0 = f0 + skip
                        hkeys = sorted(set("hb%d" % (t // 512) for t in (f0, f0 + T - 1)))
                        xkeys = sorted(set("xT%d" % (t // 512) for t in (o0, f0 + T - 1)))
                        for cc in range(nch):
                            ch = c0 + cc
                            epar = ecnt % 3
                            ecnt += 1
                            bg = 2 + 2 * epar
                            bv = 3 + 2 * epar
                            for c in range(KC):
                                mm(bank[bg][:, 0:T], Wg[:, c, cc * 128:(cc + 1) * 128], hb[:, c, f0:f0 + T],
                                   c == 0, c == KC - 1, ["ring%d" % kg] + hkeys, ["b%d" % bg], mark=(c == KC - 1))
                            for c in range(KC):
                                mm(bank[bv][:, 0:T], Wv[:, c, cc * 128:(cc + 1) * 128], hb[:, c, f0:f0 + T],
                                   c == 0, c == KC - 1, ["ring%d" % kv] + hkeys, ["b%d" % bv], mark=(c == KC - 1))
                            paths = ((bg, ch, t1[:, epar, 0:T], "t1_%d" % epar), (bv, NCH + ch, scrB[:, T2OFF[epar]:T2OFF[epar] + T], T2KEY[epar]))
                            if prompt:
                                s1 = max(skip, 1)
                                a2 = max(skip, 2)
                                for (bb, chp, Tbuf, tkey) in paths:
                                    wi = [(l * 3 + i) * 44 + chp for i in range(3)]
                                    bcol = pcol("fconv_b", l * 44 + chp)
                                    P = bank[bb][:, 0:T]
                                    act(Tbuf[:, s1:T], P[:, s1 - 1:T - 1], AF.Identity, ["b%d" % bb, "prm"], [tkey],
                                        bias=bcol, scale=pcol("fconv_w", wi[1]))
                                    if skip == 0:
                                        act(Tbuf[:, 0:1], P[:, 0:1], AF.Identity, ["b%d" % bb, "prm"], [tkey], bias=bcol, scale=0.0)
                                for (bb, chp, Tbuf, tkey) in paths:
                                    wi = [(l * 3 + i) * 44 + chp for i in range(3)]
                                    P = bank[bb][:, 0:T]
                                    stt(Tbuf[:, skip:T], P[:, skip:T], pcol("fconv_w", wi[2]), Tbuf[:, skip:T], ALU.mult, ALU.add,
                                        ["b%d" % bb, "prm", tkey], [tkey])
                                for (bb, chp, Tbuf, tkey) in paths:
                                    wi = [(l * 3 + i) * 44 + chp for i in range(3)]
                                    P = bank[bb][:, 0:T]
                                    stt(Tbuf[:, a2:T], P[:, a2 - 2:T - 2], pcol("fconv_w", wi[0]), Tbuf[:, a2:T], ALU.mult, ALU.add,
                                        ["b%d" % bb, "prm", tkey], [tkey])
                                    if fi == 4:
                                        cp(fst_p[:, chp, :], P[:, T - 2:T], ["b%d" % bb], ["fst_p"])
                            else:
                                for (bb, chp, Tbuf, tkey) in paths:
                                    wi = [(l * 3 + i) * 44 + chp for i in range(3)]
                                    bcol = pcol("fconv_b", l * 44 + chp)
                                    rk_p = ["b%d" % bb]
                                    P = bank[bb][:, 0:T].rearrange("p (s n) -> p s n", s=nseg)
                                    Tv = Tbuf.rearrange("p (s n) -> p s n", s=nseg)
                                    conv_taps(P, Tv, None, "fconv_w", wi, nseg, sl, rk_p, tkey, bcol=bcol)
                                    Hh = fst[:, chp, 0:nseg, :]
                                    hist_taps(Tv, Hh, "fconv_w", wi, "fst", tkey)
                                    cp(Hh, P[:, :, sl - 2:sl], rk_p, ["fst"])
                            act(t1[:, epar, skip:T], t1[:, epar, skip:T], AF.Silu, ["t1_%d" % epar], ["t1_%d" % epar])
                            S.op("pool", lambda e, o_=actb[:, apar, cc, 0:To], a_=t1[:, epar, skip:T],
                                 b_=scrB[:, T2OFF[epar] + skip:T2OFF[epar] + T]: e.tensor_tensor(out=o_, in0=a_, in1=b_, op=ALU.mult),
                                 reads=["t1_%d" % epar, T2KEY[epar]], writes=[akey])
                            ngrp = -(-len(pending_down) // (nch - cc))
                            for _ in range(ngrp):
                                pending_down.pop(0)()
                        def down_group(ft, nch=nch, Wd=Wd, kd=kd, apar=apar, akey=akey, To=To, o0=o0, xkeys=xkeys):
                            b = psbank()
                            for cc in range(nch):
                                mm(bank[b][:, 0:To], Wd[:, cc, ft * 128:(ft + 1) * 128], actb[:, apar, cc, 0:To],
                                   cc == 0, cc == nch - 1, ["ring%d" % kd, akey], ["b%d" % b], mark=(cc == nch - 1))
                            tt(xT[:, ft, o0:o0 + To], bank[b][:, 0:To], xT[:, ft, o0:o0 + To], ALU.add,
                               ["b%d" % b] + xkeys, xkeys)
                        while pending_down:
                            pending_down.pop(0)()
                        for ft_ in range(8):
                            pending_down.append(lambda ft_=ft_, dg=down_group: dg(ft_))
                        if fi == 0 and si + 1 < len(slices):
                            load_slice(si + 1)
                while pending_down:
                    pending_down.pop(0)()
                if has_p:
                    stq(fpo[l, sidx], fst_p[:], "fst_p")
                if has_s:
                    stq(fso[l], fst[:], "fst")

            norm_tiles(tiles, "final", 0, lambda c, t0, T: xT[:, c, t0:t0 + T], lambda ti: "xT%d" % ti)
            for (ti, t0, T, samp) in tiles:
                dst = ys[:, :, :] if samp else yp[sidx, :, :, t0:t0 + T]
                stq(dst, xT[:, :, t0:t0 + T], "xT%d" % ti)

        S.final_waits("sp")

        @block.tensor
        def _(e):
            S.emit("pe", e)

        @block.scalar
        def _(e):
            S.emit("act", e)

        @block.vector
        def _(e):
            S.emit("dve", e)

        @block.gpsimd
        def _(e):
            S.emit("pool", e)

        @block.sync
        def _(e):
            S.emit("sp", e)
    return nc


_NC_CACHE = {}


def _host_inputs(inp):
    f = lambda a: np.ascontiguousarray(np.asarray(a, dtype=np.float32))
    xpr = f(inp["x_prompt"]); xsm = f(inp["x_sample"])
    ck = f(inp["cache_attn_k"]); cv = f(inp["cache_attn_v"])
    smc = f(inp["state_mix_conv"]); sfc = f(inp["state_ffn_conv"])
    cols = []
    def addv(v):
        v = f(v).reshape(-1, 128)
        cols.append(v.T)
    addv(inp["ln1"]); addv(inp["ln2"]); addv(inp["attn_g"]); addv(inp["conv_g"])
    addv(inp["conv_w"]); addv(inp["fconv_w"]); addv(inp["fconv_b"]); addv(inp["final_norm"])
    rt = f(inp["rel_table"])
    chb = np.broadcast_to(rt[:, :, 256].reshape(1, L * NH), (128, L * NH))
    cols.append(chb)
    prm = np.ascontiguousarray(np.concatenate(cols, axis=1), dtype=np.float32)
    assert prm.shape == (128, NPRM), prm.shape
    kk = np.arange(128)[:, None, None]
    dd = np.arange(2)[None, :, None]
    qq = np.arange(128)[None, None, :]
    idx = np.clip(128 * dd + qq - kk, -128, 128) + 128
    biasT = np.ascontiguousarray(rt[:, :, idx].transpose(0, 2, 1, 3, 4))
    eye = np.eye(128, dtype=np.float32)
    shared = dict(w_in=f(inp["w_in"]), w_out=f(inp["w_out"]), w_up=f(inp["w_up"]), w_down=f(inp["w_down"]),
                  prm=prm, biasT=biasT, eye=eye)
    maps = []
    for c in range(NCORES):
        m = dict(shared)
        xp_c = xpr[2 * c:2 * c + 2].reshape(2, SEQ, KC, 128).transpose(0, 3, 2, 1)
        m["xp"] = np.ascontiguousarray(xp_c)
        xs_c = xsm[4 * c:4 * c + 4].reshape(NTS, KC, 128).transpose(2, 1, 0)
        m["xs"] = np.ascontiguousarray(xs_c)
        ck_c = ck[:, 4 * c:4 * c + 4]
        m["ckT"] = np.ascontiguousarray(ck_c.reshape(L, NSTREAM, LC, 4, 128).transpose(0, 3, 4, 1, 2))
        cv_c = cv[:, 4 * c:4 * c + 4]
        m["cvt"] = np.ascontiguousarray(
            cv_c.reshape(L, NSTREAM, 4, 128, 4, 2, 64).transpose(0, 4, 3, 1, 2, 5, 6)).reshape(L, 4, 128, NSTREAM * 8, 64)
        sm_c = smc[:, 4 * c:4 * c + 4]
        m["smc"] = np.ascontiguousarray(sm_c.reshape(L, NSTREAM, 2, 4, 128).transpose(0, 4, 3, 1, 2))
        sf_c = sfc[:, 4 * c:4 * c + 4]
        m["sfc"] = np.ascontiguousarray(sf_c.reshape(L, NSTREAM, 2, 44, 128).transpose(0, 4, 3, 1, 2))
        maps.append(m)
    return maps


def kernel(**inputs):
    if "nc" not in _NC_CACHE:
        _NC_CACHE["nc"] = build_nc()
    nc = _NC_CACHE["nc"]
    maps = _host_inputs(inputs)
    res = run_bass_kernel_spmd(nc, maps, core_ids=list(range(NCORES)))
    R = res.results
    B = 2 * NCORES
    BS = 4 * NCORES
    y_prompt = np.empty((B, SEQ, D), np.float32)
    y_sample = np.empty((BS, TS, D), np.float32)
    nk_p = np.empty((L, B, 512, NH, 64), np.float32)
    nv_p = np.empty((L, B, 512, NH, 64), np.float32)
    nc_p = np.empty((L, B, 2, 512), np.float32)
    nf_p = np.empty((L, B, 2, 2 * DFF), np.float32)
    nk_s = np.empty((L, BS, TS, NH, 64), np.float32)
    nv_s = np.empty((L, BS, TS, NH, 64), np.float32)
    nc_s = np.empty((L, BS, 2, 512), np.float32)
    nf_s = np.empty((L, BS, 2, 2 * DFF), np.float32)
    for c in range(NCORES):
        r = R[c]
        y_prompt[2 * c:2 * c + 2] = r["yp"].transpose(0, 3, 2, 1).reshape(2, SEQ, D)
        y_sample[4 * c:4 * c + 4] = r["ys"].transpose(2, 1, 0).reshape(NSTREAM, TS, D)
        nk_p[:, 2 * c:2 * c + 2] = r["kp"].transpose(0, 1, 4, 3, 2).reshape(L, 2, 512, NH, 64)
        nv_p[:, 2 * c:2 * c + 2] = r["vp"].reshape(L, 2, 512, NH, 64)
        nc_p[:, 2 * c:2 * c + 2] = r["cpo"].transpose(0, 1, 4, 3, 2).reshape(L, 2, 2, 512)
        nf_p[:, 2 * c:2 * c + 2] = r["fpo"].transpose(0, 1, 4, 3, 2).reshape(L, 2, 2, 2 * DFF)
        nk_s[:, 4 * c:4 * c + 4] = r["kso"].reshape(L, 128, 4, NSTREAM, TS).transpose(0, 3, 4, 2, 1).reshape(L, NSTREAM, TS, NH, 64)
        nv_s[:, 4 * c:4 * c + 4] = r["vso"].reshape(L, NSTREAM, TS, NH, 64)
        nc_s[:, 4 * c:4 * c + 4] = r["cso"].transpose(0, 3, 4, 2, 1).reshape(L, NSTREAM, 2, 512)
        nf_s[:, 4 * c:4 * c + 4] = r["fso"].transpose(0, 3, 4, 2, 1).reshape(L, NSTREAM, 2, 2 * DFF)
    return (y_prompt, y_sample, nk_p, nv_p, nc_p, nf_p, nk_s, nv_s, nc_s, nf_s)
```

```python
import bisect
import numpy as np
from contextlib import ExitStack
import concourse.bass as bass
import concourse.mybir as mybir
from concourse.bass_utils import run_bass_kernel_spmd

F32 = mybir.dt.float32
BF16 = mybir.dt.bfloat16
ALU = mybir.AluOpType
AF = mybir.ActivationFunctionType

NCORES = 8
L = 4
D = 1024
KC = 8
SEQ = 2048
NH = 8
DFF = 2816
NCH = 22
PROJ = 3072
EPS = 1e-6
NSTREAM = 4
TS = 16
NTS = NSTREAM * TS
LC = 512
NEG = -30000.0
UNITS = [("p", 0), ("ps", 1)]
NLAYERS = L


def _pmap():
    m = {}
    o = 0
    for name, n in (("ln1", L * 8), ("ln2", L * 8), ("attn_g", L * 4), ("conv_g", L * 4),
                    ("conv_w", L * 3 * 4), ("fconv_w", L * 3 * 44), ("fconv_b", L * 44),
                    ("final", 8), ("chb", L * 8)):
        m[name] = o
        o += n
    return m, o
PM, NPRM = _pmap()


class Sync:
    EPOCH = 12000

    def __init__(self, nc, es):
        self.nc = nc
        self.es = es
        self.eng = ["pe", "act", "dve", "pool", "sp"]
        self.prog = {e: [] for e in self.eng}
        self.seq = {e: 0 for e in self.eng}
        self.marks = {e: [] for e in self.eng}
        self.msems = {e: [] for e in self.eng}
        self.waited = {e: {} for e in self.eng}
        self.keys = {}
        self.dsem = {}

    def _newsem(self, name):
        return self.es.enter_context(self.nc.semaphore(name))

    def _k(self, key):
        if key not in self.keys:
            self.keys[key] = dict(w=None, r={}, dw=0, dr=0)
        return self.keys[key]

    def _dsem(self, key, kind):
        n = "d%s_%s" % (kind, key)
        if n not in self.dsem:
            self.dsem[n] = self._newsem(n)
        return n, self.dsem[n]

    def _mark_sem(self, e, idx):
        ep = idx // self.EPOCH
        while len(self.msems[e]) <= ep:
            self.msems[e].append(self._newsem("m_%s_%d" % (e, len(self.msems[e]))))
        return "m_%s_%d" % (e, ep), self.msems[e][ep], idx % self.EPOCH + 1

    def _need_compute(self, e, f, s, waits):
        if f == e and e == "pe":
            return
        i = bisect.bisect_left(self.marks[f], s)
        if i >= len(self.marks[f]):
            raise RuntimeError("no mark at/after seq %d on %s (needed by %s)" % (s, f, e))
        name, sem, val = self._mark_sem(f, i)
        for ep in range(i // self.EPOCH):
            self.waited[e]["m_%s_%d" % (f, ep)] = self.EPOCH
        if self.waited[e].get(name, 0) < val:
            self.waited[e][name] = val
            waits.append((sem, val))

    def _need_dma(self, e, key, kind, waits):
        st = self._k(key)
        cnt = st["dw"] if kind == "w" else st["dr"]
        if cnt == 0:
            return
        name, sem = self._dsem(key, kind)
        val = 16 * cnt
        if self.waited[e].get(name, 0) < val:
            self.waited[e][name] = val
            waits.append((sem, val))

    def _deps(self, e, reads, writes):
        waits = []
        for k in reads:
            st = self._k(k)
            if st["w"] is not None:
                self._need_compute(e, st["w"][0], st["w"][1], waits)
            self._need_dma(e, k, "w", waits)
        for k in writes:
            st = self._k(k)
            if st["w"] is not None:
                self._need_compute(e, st["w"][0], st["w"][1], waits)
            for f, s in st["r"].items():
                self._need_compute(e, f, s, waits)
            self._need_dma(e, k, "w", waits)
            self._need_dma(e, k, "r", waits)
        return waits

    @staticmethod
    def _isbank(k):
        return len(k) == 2 and k[0] == "b" and k[1].isdigit()

    def op(self, e, fn, reads=(), writes=(), mark=True):
        bk = [k for k in reads if self._isbank(k)]
        if bk:
            reads = [k for k in reads if not self._isbank(k)]
            writes = list(writes) + [k for k in bk if k not in writes]
        waits = self._deps(e, reads, writes)
        s = self.seq[e]
        self.seq[e] += 1
        for k in reads:
            self._k(k)["r"][e] = s
        for k in writes:
            st = self._k(k)
            st["w"] = (e, s)
            st["r"] = {}
        inc = None
        if mark:
            idx = len(self.marks[e])
            self.marks[e].append(s)
            _, sem, _ = self._mark_sem(e, idx)
            inc = sem
        self.prog[e].append((waits, fn, inc, 1))

    def dma(self, e, fn, reads=(), writes=()):
        waits = self._deps(e, reads, writes)
        assert len(reads) + len(writes) == 1
        if writes:
            k = writes[0]
            st = self._k(k)
            st["dw"] += 1
            st["w"] = None
            st["r"] = {}
            _, sem = self._dsem(k, "w")
        else:
            k = reads[0]
            st = self._k(k)
            st["dr"] += 1
            _, sem = self._dsem(k, "r")
        self.prog[e].append((waits, fn, sem, 16))

    def couple(self, keys_from, keys_to):
        for k2 in keys_to:
            st2 = self._k(k2)
            for k1 in keys_from:
                st1 = self._k(k1)
                for f, q in st1["r"].items():
                    st2["r"][f] = max(st2["r"].get(f, -1), q)
                if st1["w"] is not None:
                    f, q = st1["w"]
                    st2["r"][f] = max(st2["r"].get(f, -1), q)

    def final_waits(self, e):
        waits = []
        for k, st in self.keys.items():
            if st["dr"]:
                name, sem = self._dsem(k, "r")
                waits.append((sem, 16 * st["dr"]))
        self.prog[e].append((waits, None, None, 0))

    def emit(self, e, h):
        for waits, fn, inc, amt in self.prog[e]:
            for sem, val in waits:
                h.wait_ge(sem, val)
            if fn is None:
                continue
            ins = fn(h)
            if inc is not None:
                ins.then_inc(inc, amt)


def build_nc(units=UNITS, nlayers=NLAYERS):
    nc = bass.Bass("TRN2", target_bir_lowering=False)
    dt = nc.dram_tensor
    xp = dt("xp", [2, 128, KC, SEQ], F32, kind="ExternalInput").ap()
    xs = dt("xs", [128, KC, NTS], F32, kind="ExternalInput").ap()
    ckT = dt("ckT", [L, 4, 128, NSTREAM, LC], F32, kind="ExternalInput").ap()
    cvt = dt("cvt", [L, 4, 128, NSTREAM * 4 * 2, 64], F32, kind="ExternalInput").ap()
    smc = dt("smc", [L, 128, 4, NSTREAM, 2], F32, kind="ExternalInput").ap()
    sfc = dt("sfc", [L, 128, 44, NSTREAM, 2], F32, kind="ExternalInput").ap()
    w_in = dt("w_in", [L, D, PROJ], F32, kind="ExternalInput").ap()
    w_out = dt("w_out", [L, D, D], F32, kind="ExternalInput").ap()
    w_up = dt("w_up", [L, D, 2 * DFF], F32, kind="ExternalInput").ap()
    w_down = dt("w_down", [L, DFF, D], F32, kind="ExternalInput").ap()
    prm_d = dt("prm", [128, NPRM], F32, kind="ExternalInput").ap()
    bias_d = dt("biasT", [L, 128, NH, 2, 128], F32, kind="ExternalInput").ap()
    eye_d = dt("eye", [128, 128], F32, kind="ExternalInput").ap()

    yp = dt("yp", [2, 128, KC, SEQ], F32, kind="ExternalOutput").ap()
    ys = dt("ys", [128, KC, NTS], F32, kind="ExternalOutput").ap()
    kp = dt("kp", [L, 2, 128, 4, 512], F32, kind="ExternalOutput").ap()
    vp = dt("vp", [L, 2, 512, 512], F32, kind="ExternalOutput").ap()
    cpo = dt("cpo", [L, 2, 128, 4, 2], F32, kind="ExternalOutput").ap()
    fpo = dt("fpo", [L, 2, 128, 44, 2], F32, kind="ExternalOutput").ap()
    kso = dt("kso", [L, 128, 4, NTS], F32, kind="ExternalOutput").ap()
    vso = dt("vso", [L, NSTREAM, TS, 512], F32, kind="ExternalOutput").ap()
    cso = dt("cso", [L, 128, 4, NSTREAM, 2], F32, kind="ExternalOutput").ap()
    fso = dt("fso", [L, 128, 44, NSTREAM, 2], F32, kind="ExternalOutput").ap()

    es = ExitStack()
    with es:
        def sb(name, shape, dtype):
            return es.enter_context(nc.sbuf_tensor(name, shape, dtype))

        W = SEQ + NTS
        SOFF = SEQ
        xT = sb("xT", [128, KC, W], F32)
        hb = sb("hb", [128, KC, W], BF16)
        mix = sb("mix", [128, 4, W], BF16)
        qs = sb("qs", [128, 2, NTS], BF16)
        cst_p = sb("cst_p", [128, 4, 2], F32)
        fst_p = sb("fst_p", [128, 44, 2], F32)
        NSLOT = 6
        ring = sb("ring", [128, NSLOT, 4096], BF16)
        scr = sb("scr", [128, 6272], BF16)
        qTz = scr[:, 0:4096].rearrange("p (h t) -> p h t", h=2)
        kT = scr[:, 4096:6272]
        vtok = sb("vtok", [128, 17 * 2 * 65], BF16)
        vnew = sb("vnew", [16, NSTREAM, 2, 65], BF16)
        prm = sb("prm_sb", [128, NPRM], F32)
        biasb = sb("biasb", [128, 2, 2, 128], BF16)
        mask4 = sb("mask4", [128, 128], BF16)
        ident = sb("ident", [128, 128], BF16)
        ones = sb("ones", [128, 128], BF16)
        blk = sb("blk", [128, 128], BF16)
        sclv = sb("sclv", [128, 1], F32)
        stg = sb("stg", [128, 512], F32)
        sq = sb("sq", [128, 2, 512], BF16)
        osb = stg[:, 0:128].bitcast(BF16).rearrange("p (h n) -> p h n", h=2)
        scrB = sb("scrB", [128, 2064], F32)
        cu = scrB[:, 0:1040].rearrange("p (a n) -> p a n", a=2)
        cs_ = scrB[:, 1040:1552]
        zt = scrB[:, 1552:2064]
        T2OFF = [0, 520, 1040]
        T2KEY = ["cu0", "cu1", "cs"]
        cst = sb("cst", [128, 4, NSTREAM, 2], F32)
        fst = sb("fst", [128, 44, NSTREAM, 2], F32)
        t1 = sb("t1", [128, 3, 512], F32)
        t1b = t1[:, :, :].rearrange("p a n -> p (a n)").bitcast(BF16)
        pT = t1b[:, 0:1280].rearrange("p (h n) -> p h n", h=2)
        Rb = t1b[:, 1280:1536].rearrange("p (h n) -> p h n", h=2)
        rstd = t1[:, :, :].rearrange("p a n -> p (a n)")[:, 768:1024].rearrange("p (h n) -> p h n", h=2)
        sel = sb("sel", [128, 2, 128], BF16)
        sclv2 = sb("sclv2", [128, 1], F32)
        AKEYS = ["pT0", "pT1", "Rb0", "Rb1", "rstd0", "rstd1"]
        BKEYS = ["t1_0", "t1_1", "t1_2"]
        epsc = sb("epsc", [128, 1], F32)
        _b01 = [es.enter_context(nc.psum_tensor("bank%d" % i, [128, 512], F32)) for i in range(2)]
        _s23 = es.enter_context(nc.psum_tensor("bank23", [128, 1024], F32))
        _s45 = es.enter_context(nc.psum_tensor("bank45", [128, 1024], F32))
        _b67 = [es.enter_context(nc.psum_tensor("bank%d" % i, [128, 512], F32)) for i in (6, 7)]
        bank = [_b01[0][:, :], _b01[1][:, :], _s23[:, 0:512], _s23[:, 512:1024], _s45[:, 0:512], _s45[:, 512:1024],
                _b67[0][:, :], _b67[1][:, :]]
        sbig = {2: _s23, 4: _s45}

        vtp = vtok[:, :].rearrange("p (b h e) -> p b h e", b=17, h=2)
        vcs = vtok[:, 0:NSTREAM * 4 * 2 * 65].rearrange("p (s b h e) -> p s b h e", s=NSTREAM, b=4, h=2)
        actb = scr[:, 0:4096].rearrange("p (a c t) -> p a c t", a=2, c=4)

        S = Sync(nc, es)
        block = es.enter_context(nc.Block())

        def pcol(name, idx, p0=0, p1=128):
            c = PM[name] + idx
            return prm[p0:p1, c:c + 1]

        def mm(out, lhsT, rhs, start, stop, reads, writes, mark=True):
            S.op("pe", lambda e: e.matmul(out, lhsT=lhsT, rhs=rhs, start=start, stop=stop),
                 reads=reads, writes=writes, mark=mark)

        def act(out, in_, func, reads, writes, bias=None, scale=None):
            kw = {}
            if bias is not None:
                kw["bias"] = bias
            if scale is not None:
                kw["scale"] = scale
            S.op("act", lambda e: e.activation(out=out, in_=in_, func=func, **kw), reads=reads, writes=writes)

        def dve(fn, reads, writes):
            S.op("dve", fn, reads=reads, writes=writes)

        def stt(out, in0, scalar, in1, op0, op1, reads, writes):
            dve(lambda e: e.scalar_tensor_tensor(out=out, in0=in0, scalar=scalar, in1=in1, op0=op0, op1=op1),
                reads, writes)

        def tt(out, in0, in1, op, reads, writes):
            dve(lambda e: e.tensor_tensor(out=out, in0=in0, in1=in1, op=op), reads, writes)

        def cp(out, in_, reads, writes):
            dve(lambda e: e.tensor_copy(out=out, in_=in_), reads, writes)

        def recip(out, in_, reads, writes):
            dve(lambda e: e.reciprocal(out=out, in_=in_), reads, writes)

        def mset(ap, val, key):
            dve(lambda e: e.memset(ap, val), [], [key])

        def ld(out, in_, key):
            S.dma("sp", lambda e: e.dma_start(out=out, in_=in_), writes=[key])

        def stq(out, in_, key):
            S.dma("sp", lambda e: e.dma_start(out=out, in_=in_), reads=[key])

        def wld(out, in_, key):
            S.dma("pool", lambda e: e.dma_start(out=out, in_=in_), writes=[key])

        ring_next = [0]

        def ring_alloc():
            k = ring_next[0] % NSLOT
            ring_next[0] += 1
            return k

        def w_cols(wl, c0, ncols, r0=0, nkc=KC):
            return wl[r0:r0 + nkc * 128, c0:c0 + ncols].rearrange("(kc p) n -> p kc n", p=128)

        pscnt = [0]

        def psbank():
            b = pscnt[0] % 2
            pscnt[0] += 1
            return b

        ld(prm[:], prm_d[:], "prm")
        wld(ident[:], eye_d[:], "ident")
        mset(ones[:], 1.0, "ones")
        mset(blk[:], 0.0, "blk")
        mset(blk[0:64, 0:64], 1.0, "blk")
        mset(blk[64:128, 64:128], 1.0, "blk")
        mset(mask4[:], 0.0, "mask4")
        mset(mask4[0:64, 64:128], NEG, "mask4")
        mset(sel[:], 0.0, "sel")
        mset(sel[0:65, 0, 0:64], 1.0, "sel")
        mset(sel[0:65, 1, 64:128], 1.0, "sel")
        mset(epsc[:], EPS, "epsc")
        mset(sclv2[:], 1.0, "sclv2")
        mset(sclv2[64:128, :], float(64.0 * EPS), "sclv2")
        mset(sclv[:], 1.0, "sclv")
        mset(sclv[64:128, :], float(np.sqrt(64.0 * EPS)), "sclv")
        mset(vtok[:], 1.0, "vtok")
        mset(vnew[:], 1.0, "vnew")
        mset(qs[:], 0.0, "qs")

        def norm_tiles(tiles, gname, gidx0, out_fn, outkey_fn):
            for (ti, t0, T, samp) in tiles:
                for c in range(KC):
                    par = c % 2
                    act(sq[:, par, 0:T], xT[:, c, t0:t0 + T], AF.Square, ["xT%d" % ti], ["sq%d" % par])
                    mm(bank[7][:, 0:T], ones[:], sq[:, par, 0:T], c == 0, c == KC - 1,
                       ["sq%d" % par, "ones"], ["b7"])
                act(bank[7][:, 0:T], bank[7][:, 0:T], AF.Ln, ["b7", "epsc"], ["b7"], bias=epsc[:, :], scale=1.0 / D)
                act(bank[7][:, 0:T], bank[7][:, 0:T], AF.Exp, ["b7"], ["b7"], scale=-0.5)
                for c in range(KC):
                    stt(out_fn(c, t0, T), xT[:, c, t0:t0 + T], pcol(gname, gidx0 + c), bank[7][:, 0:T],
                        ALU.mult, ALU.mult, ["xT%d" % ti, "b7", "prm"], [outkey_fn(ti)])

        def head_norm(l, j, hh, N, dst):
            pb = 64 * hh
            bo = 6 + hh
            bkey = "b%d" % bo
            o_ps = bank[bo][0:65, 0:N]
            act(Rb[0:65, hh, 0:N], o_ps, AF.Square, [bkey, "sclv"], ["Rb%d" % hh], scale=sclv[0:65, :])
            st_ps = bank[bo][pb:pb + 64, 128:128 + N]
            mm(st_ps, ones[0:65, 0:64], Rb[0:65, hh, 0:N], True, True, ["Rb%d" % hh, "ones"], [bkey])
            act(st_ps, st_ps, AF.Ln, [bkey], [bkey], scale=1.0 / 64)
            act(rstd[pb:pb + 64, hh, 0:N], st_ps, AF.Exp, [bkey], ["rstd%d" % hh], scale=-0.5)
            stt(dst, bank[bo][0:64, 0:N], pcol("attn_g", l * 4 + j, pb, pb + 64),
                rstd[pb:pb + 64, hh, 0:N], ALU.mult, ALU.mult, [bkey, "rstd%d" % hh, "prm"], ["mix"])

        def conv_taps(P, Tv, Hh, wname, widx, nseg, sl, rk_p, k_t, bcol=None, first_on_act=True):
            w0 = prm[:, PM[wname] + widx[0]:PM[wname] + widx[0] + 1]
            w1 = prm[:, PM[wname] + widx[1]:PM[wname] + widx[1] + 1]
            w2 = prm[:, PM[wname] + widx[2]:PM[wname] + widx[2] + 1]
            if bcol is not None:
                act(Tv, P, AF.Identity, rk_p + ["prm"], [k_t], bias=bcol, scale=w2)
            else:
                act(Tv, P, AF.Copy, rk_p + ["prm"], [k_t], scale=w2)
            stt(Tv[:, :, 1:sl], P[:, :, 0:sl - 1], w1, Tv[:, :, 1:sl], ALU.mult, ALU.add, rk_p + ["prm", k_t], [k_t])
            stt(Tv[:, :, 2:sl], P[:, :, 0:sl - 2], w0, Tv[:, :, 2:sl], ALU.mult, ALU.add, rk_p + ["prm", k_t], [k_t])

        def hist_taps(Tv, Hh, wname, widx, hkey, k_t):
            w0 = prm[:, PM[wname] + widx[0]:PM[wname] + widx[0] + 1]
            w1 = prm[:, PM[wname] + widx[1]:PM[wname] + widx[1] + 1]
            stt(Tv[:, :, 0:1], Hh[:, :, 1:2], w1, Tv[:, :, 0:1], ALU.mult, ALU.add, [hkey, "prm", k_t], [k_t])
            stt(Tv[:, :, 0:2], Hh[:, :, 0:2], w0, Tv[:, :, 0:2], ALU.mult, ALU.add, [hkey, "prm", k_t], [k_t])

        for (kind, sidx) in units:
            has_p = "p" in kind
            has_s = "s" in kind
            tiles = []
            if has_p:
                tiles += [(i, i * 512, 512, False) for i in range(4)]
            if has_s:
                tiles += [(4, SOFF, NTS, True)]
            parts = (["p"] if has_p else []) + (["s"] if has_s else [])
            for (ti, t0, T, samp) in tiles:
                src = xs[:, :, :] if samp else xp[sidx, :, :, t0:t0 + T]
                ld(xT[:, :, t0:t0 + T], src, "xT%d" % ti)

            for l in range(nlayers):
                import os as _os
                _sk = _os.environ.get("DBG_SKIP", "") if l >= 1 else ""
                if has_s:
                    ld(cst[:], smc[l], "cst")
                    ld(fst[:], sfc[l], "fst")

                import os as _os
                _dbg = _os.environ.get("DBG_STOP", "") if (l >= 1 or _os.environ.get("DBG_L0")) else ""
                if _dbg == "loads":
                    continue
                norm_tiles(tiles, "ln1", l * 8, lambda c, t0, T: hb[:, c, t0:t0 + T], lambda ti: "hb%d" % ti)

                if _dbg == "norm1":
                    continue
                pend_norm = []
                cvn = [0]
                cbn = [0]

                def cbank():
                    b_ = cbn[0] % 6
                    cbn[0] += 1
                    return b_
                for jc in range(4):
                    k = ring_alloc()
                    rk = "ring%d" % k
                    wc = ring[:, k, 0:1024].rearrange("p (kc n) -> p kc n", kc=KC)
                    wu = ring[:, k, 1024:2048].rearrange("p (kc n) -> p kc n", kc=KC)
                    wb = ring[:, k, 2048:3072].rearrange("p (kc n) -> p kc n", kc=KC)
                    wld(wc, w_cols(w_in[l], 2048 + jc * 128, 128), rk)
                    wld(wu, w_cols(w_in[l], 2560 + jc * 128, 128), rk)
                    wld(wb, w_cols(w_in[l], 1536 + jc * 128, 128), rk)
                    for (ti, t0, T, samp) in tiles:
                        prompt = not samp
                        nseg, sl = (NSTREAM, TS) if samp else (1, 512)
                        par = ti % 2
                        cuv = cu[:, par, 0:nseg * (sl + 2)].rearrange("p (s n) -> p s n", s=nseg)
                        b = cbank()
                        for c in range(KC):
                            mm(bank[b][:, 0:T], wc[:, c, :], hb[:, c, t0:t0 + T], c == 0, c == KC - 1,
                               [rk, "hb%d" % ti], ["b%d" % b], mark=(c == KC - 1))
                        act(cs_[:, 0:T], bank[b][:, 0:T], AF.Copy, ["b%d" % b], ["cs"])
                        b = cbank()
                        for c in range(KC):
                            mm(bank[b][:, 0:T], wu[:, c, :], hb[:, c, t0:t0 + T], c == 0, c == KC - 1,
                               [rk, "hb%d" % ti], ["b%d" % b], mark=(c == KC - 1))
                        if prompt:
                            if ti == 0:
                                mset(cuv[:, :, 0:2], 0.0, "cu%d" % par)
                            else:
                                cp(cuv[:, :, 0:2], cu[:, 1 - par, 512:514].rearrange("p (s n) -> p s n", s=1),
                                   ["cu%d" % (1 - par)], ["cu%d" % par])
                        else:
                            cp(cuv[:, :, 0:2], cst[:, jc, :, :], ["cst"], ["cu%d" % par])
                        tt(cuv[:, :, 2:sl + 2], bank[b][:, 0:T].rearrange("p (s n) -> p s n", s=nseg),
                           cs_[:, 0:T].rearrange("p (s n) -> p s n", s=nseg), ALU.mult,
                           ["b%d" % b, "cs"], ["cu%d" % par])
                        if prompt and ti == 3:
                            cp(cst_p[:, jc:jc + 1, :], cuv[:, :, sl:sl + 2], ["cu%d" % par], ["cst_p"])
                        if samp:
                            cp(cst[:, jc, :, :], cuv[:, :, sl:sl + 2], ["cu%d" % par], ["cst"])
                        zpar = cvn[0] % 2
                        cvn[0] += 1
                        zbuf = zt if zpar == 0 else stg
                        zkey = "zt" if zpar == 0 else "stg"
                        ztv = zbuf[:, 0:T].rearrange("p (s n) -> p s n", s=nseg)
                        wi = [(l * 3 + i) * 4 + jc for i in range(3)]
                        w0 = pcol("conv_w", wi[0]); w1 = pcol("conv_w", wi[1]); w2 = pcol("conv_w", wi[2])
                        act(ztv, cuv[:, :, 2:sl + 2], AF.Copy, ["cu%d" % par, "prm"], [zkey], scale=w2)
                        stt(ztv, cuv[:, :, 1:sl + 1], w1, ztv, ALU.mult, ALU.add, ["cu%d" % par, "prm", zkey], [zkey])
                        stt(ztv, cuv[:, :, 0:sl], w0, ztv, ALU.mult, ALU.add, ["cu%d" % par, "prm", zkey], [zkey])
                        b = cbank()
                        for c in range(KC):
                            mm(bank[b][:, 0:T], wb[:, c, :], hb[:, c, t0:t0 + T], c == 0, c == KC - 1,
                               [rk, "hb%d" % ti], ["b%d" % b], mark=(c == KC - 1))
                        tt(zbuf[:, 0:T], bank[b][:, 0:T], zbuf[:, 0:T], ALU.mult, ["b%d" % b, zkey], [zkey])

                        def norm_stage(zbuf=zbuf, zkey=zkey, zpar=zpar, T=T, t0=t0, jc=jc, l=l):
                            act(sq[:, zpar, 0:T], zbuf[:, 0:T], AF.Square, [zkey], ["sq%d" % zpar])
                            mm(bank[7][:, 0:T], blk[:], sq[:, zpar, 0:T], True, True, ["sq%d" % zpar, "blk"], ["b7"])
                            act(bank[7][:, 0:T], bank[7][:, 0:T], AF.Ln, ["b7", "epsc"], ["b7"], bias=epsc[:, :], scale=1.0 / 64)
                            act(bank[7][:, 0:T], bank[7][:, 0:T], AF.Exp, ["b7"], ["b7"], scale=-0.5)
                            stt(mix[:, jc, t0:t0 + T], zbuf[:, 0:T], pcol("conv_g", l * 4 + jc), bank[7][:, 0:T],
                                ALU.mult, ALU.mult, [zkey, "b7", "prm"], ["mix"])
                        if pend_norm:
                            pend_norm.pop(0)()
                        pend_norm.append(norm_stage)
                while pend_norm:
                    pend_norm.pop(0)()
                if has_p:
                    stq(cpo[l, sidx], cst_p[:], "cst_p")
                if has_s:
                    stq(cso[l], cst[:], "cst")

                def wout_part(r0):
                    k = ring_alloc()
                    rk = "ring%d" % k
                    wv_ = ring[:, k, 0:4096].rearrange("p (kc n) -> p kc n", kc=4)
                    wld(wv_, w_cols(w_out[l], 0, D, r0=r0, nkc=4), rk)
                    for (ti, t0, T, samp) in tiles:
                        for ft in range(8):
                            b = psbank()
                            for c in range(4):
                                mm(bank[b][:, 0:T], wv_[:, c, ft * 128:(ft + 1) * 128], mix[:, c, t0:t0 + T],
                                   c == 0, c == 3, [rk, "mix"], ["b%d" % b], mark=(c == 3))
                            tt(xT[:, ft, t0:t0 + T], bank[b][:, 0:T], xT[:, ft, t0:t0 + T], ALU.add,
                               ["b%d" % b, "xT%d" % ti], ["xT%d" % ti])

                if _dbg == "conv":
                    continue
                wout_part(512)
                if _dbg == "wout1":
                    continue

                S.couple(BKEYS, AKEYS)
                if has_p:
                    S.op("pool", lambda e: e.memset(qTz[64:128, 0, 0:SEQ], 0.0), writes=["qT0"])
                    S.op("pool", lambda e: e.memset(qTz[0:64, 1, 0:SEQ], 0.0), writes=["qT1"])
                for j in range(4):
                    k = ring_alloc()
                    rk = "ring%d" % k
                    wq = ring[:, k, 0:1024].rearrange("p (kc n) -> p kc n", kc=KC)
                    wk = ring[:, k, 1024:2048].rearrange("p (kc n) -> p kc n", kc=KC)
                    wv = ring[:, k, 2048:3072].rearrange("p (kc n) -> p kc n", kc=KC)
                    wld(wq, w_cols(w_in[l], j * 128, 128), rk)
                    wld(wk, w_cols(w_in[l], 512 + j * 128, 128), rk)
                    wld(wv, w_cols(w_in[l], 1024 + j * 128, 128), rk)
                    sv = stg[:, :].rearrange("p (h n) -> p h n", h=2)
                    ld(sv, bias_d[l, :, 2 * j:2 * j + 2, :, :].rearrange("p h d n -> p h (d n)"), "stg")
                    for h2 in range(2):
                        dve(lambda e, h2=h2, sv=sv, l=l, j=j: e.tensor_scalar(
                            out=biasb[:, h2, :, :].rearrange("p d n -> p (d n)"), in0=sv[:, h2, :],
                            scalar1=pcol("chb", l * 8 + 2 * j + h2), scalar2=None, op0=ALU.subtract),
                            ["stg", "prm"], ["biasb"])
                        mset(biasb[64:128, h2, 0, 0:64], NEG, "biasb")
                    for part in parts:
                        prompt = part == "p"
                        ptiles = [t_ for t_ in tiles if t_[3] != prompt]
                        koff = 0 if prompt else NSTREAM * LC
                        _dl = _os.environ.get("DBG_LD", "")
                        if not prompt and "nokt" not in _dl:
                            wld(kT[:, 0:NSTREAM * LC].rearrange("p (s n) -> p s n", s=NSTREAM), ckT[l, j], "kT")
                        if not prompt and "novcs" not in _dl:
                            wld(vtok[:, 0:NSTREAM * 4 * 2 * 65].rearrange("p (g e) -> p g e", e=65)[:, :, 0:64], cvt[l, j], "vtok")
                        for (ti, t0, T, samp) in ptiles:
                            lo = t0 - SOFF if samp else t0
                            b = psbank()
                            for c in range(KC):
                                mm(bank[b][:, 0:T], wq[:, c, :], hb[:, c, t0:t0 + T], c == 0, c == KC - 1,
                                   [rk, "hb%d" % ti], ["b%d" % b], mark=(c == KC - 1))
                            if prompt:
                                act(qTz[0:64, 0, t0:t0 + T], bank[b][0:64, 0:T], AF.Copy, ["b%d" % b], ["qT0"], scale=0.125)
                                act(qTz[64:128, 1, t0:t0 + T], bank[b][64:128, 0:T], AF.Copy, ["b%d" % b], ["qT1"], scale=0.125)
                            else:
                                act(qs[0:64, 0, 0:T], bank[b][0:64, 0:T], AF.Copy, ["b%d" % b], ["qs"], scale=0.125)
                                act(qs[64:128, 1, 0:T], bank[b][64:128, 0:T], AF.Copy, ["b%d" % b], ["qs"], scale=0.125)
                            b = psbank()
                            for c in range(KC):
                                mm(bank[b][:, 0:T], wk[:, c, :], hb[:, c, t0:t0 + T], c == 0, c == KC - 1,
                                   [rk, "hb%d" % ti], ["b%d" % b], mark=(c == KC - 1))
                            act(kT[:, koff + lo:koff + lo + T], bank[b][:, 0:T], AF.Copy, ["b%d" % b], ["kT"])
                            if (not prompt) or ti == 3:
                                cp(stg[:, 0:T], bank[b][:, 0:T], ["b%d" % b], ["stg"])
                                if prompt:
                                    stq(kp[l, sidx, :, j, :], stg[:, 0:T], "stg")
                                else:
                                    stq(kso[l, :, j, :], stg[:, 0:T], "stg")
                            if "nov" in _dl:
                                continue
                            if prompt:
                                for tb in range(4):
                                    g = ti * 4 + tb
                                    b = psbank()
                                    for c in range(KC):
                                        mm(bank[b][:, 0:128], hb[:, c, t0 + tb * 128:t0 + (tb + 1) * 128], wv[:, c, :],
                                           c == 0, c == KC - 1, [rk, "hb%d" % ti], ["b%d" % b], mark=(c == KC - 1))
                                    act(vtp[:, g, :, 0:64], bank[b][:, 0:128].rearrange("p (h e) -> p h e", h=2),
                                        AF.Copy, ["b%d" % b], ["vtok"])
                                    if ti == 3:
                                        cp(stg[:, 0:128], bank[b][:, 0:128], ["b%d" % b], ["stg"])
                                        stq(vp[l, sidx, tb * 128:(tb + 1) * 128, j * 128:(j + 1) * 128], stg[:, 0:128], "stg")
                            else:
                                for s_ in range(NSTREAM):
                                    b = psbank()
                                    for c in range(KC):
                                        mm(bank[b][0:TS, 0:128], hb[:, c, t0 + s_ * TS:t0 + (s_ + 1) * TS], wv[:, c, :],
                                           c == 0, c == KC - 1, [rk, "hb%d" % ti], ["b%d" % b], mark=(c == KC - 1))
                                    act(vnew[0:TS, s_, :, 0:64], bank[b][0:TS, 0:128].rearrange("p (h e) -> p h e", h=2),
                                        AF.Copy, ["b%d" % b], ["vnew"])
                                    cp(stg[0:TS, 0:128], bank[b][0:TS, 0:128], ["b%d" % b], ["stg"])
                                    stq(vso[l, s_, :, j * 128:(j + 1) * 128], stg[0:TS, 0:128], "stg")
                        _da = _os.environ.get("DBG_ATT", "")
                        if _da == "noattn":
                            continue
                        if prompt:
                            def stA(i, hh):
                                nkb = min(i, 4) + 1
                                h = 2 * j + hh
                                pb = 64 * hh
                                for d in range(nkb):
                                    jj = i - d
                                    bk = 2 + 2 * hh + d // 4
                                    o_ = bank[bk][:, (d % 4) * 128:(d % 4) * 128 + 128]
                                    hasb = d in (0, 1, 4)
                                    mm(o_, kT[:, jj * 128:(jj + 1) * 128], qTz[:, hh, i * 128:(i + 1) * 128],
                                       True, not hasb, ["kT", "qT%d" % hh], ["b%d" % bk], mark=not hasb)
                                    if hasb:
                                        rb_ = mask4[:, :] if d == 4 else biasb[:, hh, d, :]
                                        mm(o_, ident[:], rb_, False, True, ["ident", "biasb", "mask4"], ["b%d" % bk])
                                n0 = nkb * 128
                                bk = 2 + 2 * hh
                                rk_ = ["b%d" % bk] + (["b%d" % (bk + 1)] if nkb == 5 else [])
                                act(pT[:, hh, 0:n0], sbig[bk][:, 0:n0], AF.Exp, rk_, ["pT%d" % hh])

                            PVB = [0, 1, 6, 7]

                            def stB(i, hh):
                                nkb = min(i, 4) + 1
                                bo = PVB[(2 * i + hh) % 4]
                                for d in range(nkb):
                                    jj = i - d
                                    mm(bank[bo][0:65, 0:128], vtp[:, jj, hh, :],
                                       pT[:, hh, d * 128:(d + 1) * 128], d == 0, d == nkb - 1,
                                       ["vtok", "pT%d" % hh], ["b%d" % bo], mark=(d == nkb - 1))
                                cp(osb[0:65, hh, :], bank[bo][0:65, 0:128], ["b%d" % bo], ["stg"])
                                stt(Rb[0:65, hh, 0:128], bank[bo][0:65, 0:128], sclv2[0:65, :], osb[0:65, hh, :],
                                    ALU.mult, ALU.mult, ["b%d" % bo, "stg", "sclv2"], ["Rb%d" % hh])

                            def stC(i, hh):
                                pb = 64 * hh
                                bo = PVB[(2 * i + hh) % 4]
                                bkey = "b%d" % bo
                                st_ps = bank[bo][pb:pb + 64, 128:256]
                                mm(bank[bo][:, 128:256], sel[0:65, hh, :], Rb[0:65, hh, 0:128], True, True, ["Rb%d" % hh, "sel"], [bkey])
                                act(st_ps, st_ps, AF.Ln, [bkey], [bkey], scale=1.0 / 64)
                                act(rstd[pb:pb + 64, hh, 0:128], st_ps, AF.Exp, [bkey], ["rstd%d" % hh], scale=-0.5)
                                stt(mix[pb:pb + 64, j, i * 128:(i + 1) * 128], bank[bo][0:64, 0:128],
                                    pcol("attn_g", l * 4 + j, pb, pb + 64), rstd[pb:pb + 64, hh, 0:128],
                                    ALU.mult, ALU.mult, [bkey, "rstd%d" % hh, "prm"], ["mix"])

                            seq_ = [(i, hh) for i in range(16) for hh in range(2)]
                            for n in range(len(seq_) + 2):
                                if n < len(seq_):
                                    stA(*seq_[n])
                                if 0 <= n - 1 < len(seq_):
                                    stB(*seq_[n - 1])
                                if 0 <= n - 2 < len(seq_):
                                    stC(*seq_[n - 2])
                        else:
                            for hh in range(2):
                                h = 2 * j + hh
                                pb = 64 * hh
                                bk = 2 + 2 * hh
                                Sv = bank[bk][:, 0:NSTREAM * 5 * TS].rearrange("p (s b q) -> p s b q", s=NSTREAM, b=5)
                                pv = pT[:, hh, 0:NSTREAM * 5 * TS].rearrange("p (s b q) -> p s b q", s=NSTREAM, b=5)
                                for s_ in range(NSTREAM):
                                    qs_ = qs[pb:pb + 64, hh, s_ * TS:(s_ + 1) * TS]
                                    for kb in range(4):
                                        mm(Sv[:, s_, kb, :], kT[pb:pb + 64, s_ * LC + kb * 128:s_ * LC + (kb + 1) * 128], qs_,
                                           True, kb != 3, ["kT", "qs"], ["b%d" % bk], mark=(kb != 3))
                                        if kb == 3:
                                            mm(Sv[:, s_, kb, :], ident[:], biasb[:, hh, 1, 0:TS], False, True,
                                               ["ident", "biasb"], ["b%d" % bk])
                                    mm(Sv[0:TS, s_, 4, :], kT[pb:pb + 64, NSTREAM * LC + s_ * TS:NSTREAM * LC + (s_ + 1) * TS], qs_,
                                       True, False, ["kT", "qs"], ["b%d" % bk], mark=False)
                                    mm(Sv[0:TS, s_, 4, :], ident[0:TS, 0:TS], biasb[0:TS, hh, 0, 0:TS], False, True,
                                       ["ident", "biasb"], ["b%d" % bk])
                                if _da == "S":
                                    continue
                                act(pv[:, :, 0:4, :], Sv[:, :, 0:4, :], AF.Exp, ["b%d" % bk], ["pT%d" % hh])
                                act(pv[0:TS, :, 4:5, :], Sv[0:TS, :, 4:5, :], AF.Exp, ["b%d" % bk], ["pT%d" % hh])
                                if _da == "exp":
                                    continue
                                for s_ in range(NSTREAM):
                                    o_ = bank[6 + hh][0:65, s_ * TS:(s_ + 1) * TS]
                                    for kb in range(4):
                                        mm(o_, vcs[:, s_, kb, hh, :], pv[:, s_, kb, :], kb == 0, False,
                                           ["vtok", "pT%d" % hh], ["b%d" % (6 + hh)], mark=False)
                                    mm(o_, vnew[0:TS, s_, hh, :], pv[0:TS, s_, 4, :], False, True,
                                       ["vnew", "pT%d" % hh], ["b%d" % (6 + hh)])
                                if _da == "pv":
                                    continue
                                head_norm(l, j, hh, NTS, mix[pb:pb + 64, j, SOFF:SOFF + NTS])

                if _dbg == "attn":
                    continue
                wout_part(0)
                if _dbg == "wout0":
                    continue

                S.couple(AKEYS, BKEYS)
                norm_tiles(tiles, "ln2", l * 8, lambda c, t0, T: hb[:, c, t0:t0 + T], lambda ti: "hb%d" % ti)
                slices = [(0, 4), (4, 4), (8, 4), (12, 4), (16, 4), (20, 2)]
                if _dbg == "norm2":
                    continue
                if _dbg == "ffn1":
                    slices = slices[:1]
                acnt = 0
                ecnt = 0
                pending_down = []
                ftiles = []
                if has_p:
                    ftiles += [(0, 512, 0, False), (510, 512, 2, False), (1020, 512, 2, False), (1530, 456, 2, False),
                               (1984, 64, 2, False)]
                if has_s:
                    ftiles += [(SOFF, NTS, 0, True)]
                sl_w = {}

                def load_slice(si):
                    c0_, nch_ = slices[si]
                    kg_ = ring_alloc(); kv_ = ring_alloc(); kd_ = ring_alloc()
                    Wg_ = ring[:, kg_, 0:KC * nch_ * 128].rearrange("p (kc n) -> p kc n", kc=KC)
                    Wv_ = ring[:, kv_, 0:KC * nch_ * 128].rearrange("p (kc n) -> p kc n", kc=KC)
                    Wd_ = ring[:, kd_, 0:nch_ * D].rearrange("p (c n) -> p c n", c=nch_)
                    wld(Wg_, w_cols(w_up[l], c0_ * 128, nch_ * 128), "ring%d" % kg_)
                    wld(Wv_, w_cols(w_up[l], DFF + c0_ * 128, nch_ * 128), "ring%d" % kv_)
                    wld(Wd_, w_down[l, c0_ * 128:(c0_ + nch_) * 128, :].rearrange("(c p) n -> p c n", p=128), "ring%d" % kd_)
                    sl_w[si] = (kg_, kv_, kd_, Wg_, Wv_, Wd_)

                load_slice(0)
                for si, (c0, nch) in enumerate(slices):
                    kg, kv, kd, Wg, Wv, Wd = sl_w[si]
                    for fi, (f0, T, skip, samp) in enumerate(ftiles):
                        prompt = not samp
                        nseg, sl = (NSTREAM, TS) if samp else (1, 512)
                        apar = acnt % 2
                        acnt += 1
                        akey = "qT%d" % apar
                        To = T - skip
                        o0 = f0 + skip
                        hkeys = sorted(set("hb%d" % (t // 512) for t in (f0, f0 + T - 1)))
                        xkeys = sorted(set("xT%d" % (t // 512) for t in (o0, f0 + T - 1)))
                        for cc in range(nch):
                            ch = c0 + cc
                            epar = ecnt % 3
                            ecnt += 1
                            bg = 2 + 2 * epar
                            bv = 3 + 2 * epar
                            for c in range(KC):
                                mm(bank[bg][:, 0:T], Wg[:, c, cc * 128:(cc + 1) * 128], hb[:, c, f0:f0 + T],
                                   c == 0, c == KC - 1, ["ring%d" % kg] + hkeys, ["b%d" % bg], mark=(c == KC - 1))
                            for c in range(KC):
                                mm(bank[bv][:, 0:T], Wv[:, c, cc * 128:(cc + 1) * 128], hb[:, c, f0:f0 + T],
                                   c == 0, c == KC - 1, ["ring%d" % kv] + hkeys, ["b%d" % bv], mark=(c == KC - 1))
                            paths = ((bg, ch, t1[:, epar, 0:T], "t1_%d" % epar), (bv, NCH + ch, scrB[:, T2OFF[epar]:T2OFF[epar] + T], T2KEY[epar]))
                            if prompt:
                                s1 = max(skip, 1)
                                a2 = max(skip, 2)
                                for (bb, chp, Tbuf, tkey) in paths:
                                    wi = [(l * 3 + i) * 44 + chp for i in range(3)]
                                    bcol = pcol("fconv_b", l * 44 + chp)
                                    P = bank[bb][:, 0:T]
                                    act(Tbuf[:, s1:T], P[:, s1 - 1:T - 1], AF.Identity, ["b%d" % bb, "prm"], [tkey],
                                        bias=bcol, scale=pcol("fconv_w", wi[1]))
                                    if skip == 0:
                                        act(Tbuf[:, 0:1], P[:, 0:1], AF.Identity, ["b%d" % bb, "prm"], [tkey], bias=bcol, scale=0.0)
                                for (bb, chp, Tbuf, tkey) in paths:
                                    wi = [(l * 3 + i) * 44 + chp for i in range(3)]
                                    P = bank[bb][:, 0:T]
                                    stt(Tbuf[:, skip:T], P[:, skip:T], pcol("fconv_w", wi[2]), Tbuf[:, skip:T], ALU.mult, ALU.add,
                                        ["b%d" % bb, "prm", tkey], [tkey])
                                for (bb, chp, Tbuf, tkey) in paths:
                                    wi = [(l * 3 + i) * 44 + chp for i in range(3)]
                                    P = bank[bb][:, 0:T]
                                    stt(Tbuf[:, a2:T], P[:, a2 - 2:T - 2], pcol("fconv_w", wi[0]), Tbuf[:, a2:T], ALU.mult, ALU.add,
                                        ["b%d" % bb, "prm", tkey], [tkey])
                                    if fi == 4:
                                        cp(fst_p[:, chp, :], P[:, T - 2:T], ["b%d" % bb], ["fst_p"])
                            else:
                                for (bb, chp, Tbuf, tkey) in paths:
                                    wi = [(l * 3 + i) * 44 + chp for i in range(3)]
                                    bcol = pcol("fconv_b", l * 44 + chp)
                                    rk_p = ["b%d" % bb]
                                    P = bank[bb][:, 0:T].rearrange("p (s n) -> p s n", s=nseg)
                                    Tv = Tbuf.rearrange("p (s n) -> p s n", s=nseg)
                                    conv_taps(P, Tv, None, "fconv_w", wi, nseg, sl, rk_p, tkey, bcol=bcol)
                                    Hh = fst[:, chp, 0:nseg, :]
                                    hist_taps(Tv, Hh, "fconv_w", wi, "fst", tkey)
                                    cp(Hh, P[:, :, sl - 2:sl], rk_p, ["fst"])
                            act(t1[:, epar, skip:T], t1[:, epar, skip:T], AF.Silu, ["t1_%d" % epar], ["t1_%d" % epar])
                            S.op("pool", lambda e, o_=actb[:, apar, cc, 0:To], a_=t1[:, epar, skip:T],
                                 b_=scrB[:, T2OFF[epar] + skip:T2OFF[epar] + T]: e.tensor_tensor(out=o_, in0=a_, in1=b_, op=ALU.mult),
                                 reads=["t1_%d" % epar, T2KEY[epar]], writes=[akey])
                            ngrp = -(-len(pending_down) // (nch - cc))
                            for _ in range(ngrp):
                                pending_down.pop(0)()
                        def down_group(ft, nch=nch, Wd=Wd, kd=kd, apar=apar, akey=akey, To=To, o0=o0, xkeys=xkeys):
                            b = psbank()
                            for cc in range(nch):
                                mm(bank[b][:, 0:To], Wd[:, cc, ft * 128:(ft + 1) * 128], actb[:, apar, cc, 0:To],
                                   cc == 0, cc == nch - 1, ["ring%d" % kd, akey], ["b%d" % b], mark=(cc == nch - 1))
                            tt(xT[:, ft, o0:o0 + To], bank[b][:, 0:To], xT[:, ft, o0:o0 + To], ALU.add,
                               ["b%d" % b] + xkeys, xkeys)
                        while pending_down:
                            pending_down.pop(0)()
                        for ft_ in range(8):
                            pending_down.append(lambda ft_=ft_, dg=down_group: dg(ft_))
                        if fi == 0 and si + 1 < len(slices):
                            load_slice(si + 1)
                while pending_down:
                    pending_down.pop(0)()
                if has_p:
                    stq(fpo[l, sidx], fst_p[:], "fst_p")
                if has_s:
                    stq(fso[l], fst[:], "fst")

            norm_tiles(tiles, "final", 0, lambda c, t0, T: xT[:, c, t0:t0 + T], lambda ti: "xT%d" % ti)
            for (ti, t0, T, samp) in tiles:
                dst = ys[:, :, :] if samp else yp[sidx, :, :, t0:t0 + T]
                stq(dst, xT[:, :, t0:t0 + T], "xT%d" % ti)

        S.final_waits("sp")

        @block.tensor
        def _(e):
            S.emit("pe", e)

        @block.scalar
        def _(e):
            S.emit("act", e)

        @block.vector
        def _(e):
            S.emit("dve", e)

        @block.gpsimd
        def _(e):
            S.emit("pool", e)

        @block.sync
        def _(e):
            S.emit("sp", e)
    return nc


_NC_CACHE = {}


def _host_inputs(inp):
    f = lambda a: np.ascontiguousarray(np.asarray(a, dtype=np.float32))
    xpr = f(inp["x_prompt"]); xsm = f(inp["x_sample"])
    ck = f(inp["cache_attn_k"]); cv = f(inp["cache_attn_v"])
    smc = f(inp["state_mix_conv"]); sfc = f(inp["state_ffn_conv"])
    cols = []
    def addv(v):
        v = f(v).reshape(-1, 128)
        cols.append(v.T)
    addv(inp["ln1"]); addv(inp["ln2"]); addv(inp["attn_g"]); addv(inp["conv_g"])
    addv(inp["conv_w"]); addv(inp["fconv_w"]); addv(inp["fconv_b"]); addv(inp["final_norm"])
    rt = f(inp["rel_table"])
    chb = np.broadcast_to(rt[:, :, 256].reshape(1, L * NH), (128, L * NH))
    cols.append(chb)
    prm = np.ascontiguousarray(np.concatenate(cols, axis=1), dtype=np.float32)
    assert prm.shape == (128, NPRM), prm.shape
    kk = np.arange(128)[:, None, None]
    dd = np.arange(2)[None, :, None]
    qq = np.arange(128)[None, None, :]
    idx = np.clip(128 * dd + qq - kk, -128, 128) + 128
    biasT = np.ascontiguousarray(rt[:, :, idx].transpose(0, 2, 1, 3, 4))
    eye = np.eye(128, dtype=np.float32)
    shared = dict(w_in=f(inp["w_in"]), w_out=f(inp["w_out"]), w_up=f(inp["w_up"]), w_down=f(inp["w_down"]),
                  prm=prm, biasT=biasT, eye=eye)
    maps = []
    for c in range(NCORES):
        m = dict(shared)
        xp_c = xpr[2 * c:2 * c + 2].reshape(2, SEQ, KC, 128).transpose(0, 3, 2, 1)
        m["xp"] = np.ascontiguousarray(xp_c)
        xs_c = xsm[4 * c:4 * c + 4].reshape(NTS, KC, 128).transpose(2, 1, 0)
        m["xs"] = np.ascontiguousarray(xs_c)
        ck_c = ck[:, 4 * c:4 * c + 4]
        m["ckT"] = np.ascontiguousarray(ck_c.reshape(L, NSTREAM, LC, 4, 128).transpose(0, 3, 4, 1, 2))
        cv_c = cv[:, 4 * c:4 * c + 4]
        m["cvt"] = np.ascontiguousarray(
            cv_c.reshape(L, NSTREAM, 4, 128, 4, 2, 64).transpose(0, 4, 3, 1, 2, 5, 6)).reshape(L, 4, 128, NSTREAM * 8, 64)
        sm_c = smc[:, 4 * c:4 * c + 4]
        m["smc"] = np.ascontiguousarray(sm_c.reshape(L, NSTREAM, 2, 4, 128).transpose(0, 4, 3, 1, 2))
        sf_c = sfc[:, 4 * c:4 * c + 4]
        m["sfc"] = np.ascontiguousarray(sf_c.reshape(L, NSTREAM, 2, 44, 128).transpose(0, 4, 3, 1, 2))
        maps.append(m)
    return maps


def kernel(**inputs):
    if "nc" not in _NC_CACHE:
        _NC_CACHE["nc"] = build_nc()
    nc = _NC_CACHE["nc"]
    maps = _host_inputs(inputs)
    res = run_bass_kernel_spmd(nc, maps, core_ids=list(range(NCORES)))
    R = res.results
    B = 2 * NCORES
    BS = 4 * NCORES
    y_prompt = np.empty((B, SEQ, D), np.float32)
    y_sample = np.empty((BS, TS, D), np.float32)
    nk_p = np.empty((L, B, 512, NH, 64), np.float32)
    nv_p = np.empty((L, B, 512, NH, 64), np.float32)
    nc_p = np.empty((L, B, 2, 512), np.float32)
    nf_p = np.empty((L, B, 2, 2 * DFF), np.float32)
    nk_s = np.empty((L, BS, TS, NH, 64), np.float32)
    nv_s = np.empty((L, BS, TS, NH, 64), np.float32)
    nc_s = np.empty((L, BS, 2, 512), np.float32)
    nf_s = np.empty((L, BS, 2, 2 * DFF), np.float32)
    for c in range(NCORES):
        r = R[c]
        y_prompt[2 * c:2 * c + 2] = r["yp"].transpose(0, 3, 2, 1).reshape(2, SEQ, D)
        y_sample[4 * c:4 * c + 4] = r["ys"].transpose(2, 1, 0).reshape(NSTREAM, TS, D)
        nk_p[:, 2 * c:2 * c + 2] = r["kp"].transpose(0, 1, 4, 3, 2).reshape(L, 2, 512, NH, 64)
        nv_p[:, 2 * c:2 * c + 2] = r["vp"].reshape(L, 2, 512, NH, 64)
        nc_p[:, 2 * c:2 * c + 2] = r["cpo"].transpose(0, 1, 4, 3, 2).reshape(L, 2, 2, 512)
        nf_p[:, 2 * c:2 * c + 2] = r["fpo"].transpose(0, 1, 4, 3, 2).reshape(L, 2, 2, 2 * DFF)
        nk_s[:, 4 * c:4 * c + 4] = r["kso"].reshape(L, 128, 4, NSTREAM, TS).transpose(0, 3, 4, 2, 1).reshape(L, NSTREAM, TS, NH, 64)
        nv_s[:, 4 * c:4 * c + 4] = r["vso"].reshape(L, NSTREAM, TS, NH, 64)
        nc_s[:, 4 * c:4 * c + 4] = r["cso"].transpose(0, 3, 4, 2, 1).reshape(L, NSTREAM, 2, 512)
        nf_s[:, 4 * c:4 * c + 4] = r["fso"].transpose(0, 3, 4, 2, 1).reshape(L, NSTREAM, 2, 2 * DFF)
    return (y_prompt, y_sample, nk_p, nv_p, nc_p, nf_p, nk_s, nv_s, nc_s, nf_s)
```
